# Optimizing a Trainium2 kernel written in Bass

```python
import math
import jax, jax.numpy as jnp
from jax import lax
import numpy as np

D_MODEL = 2048
BATCH = 4
SEQ = 4096
DEPTH = 1
DEC_BATCH = 2
DEC_SEQ = 8192
PAST_LEN = 128

N_MEM = 256
DA_HEADS = 8
DA_DIM = 128
DA_V = 2 * DA_DIM
DA_QK_W = DA_HEADS * 2 * DA_DIM
DA_V_W = DA_HEADS * DA_V
Q_BLOCK = 128
GDN_HEADS = 16
GDN_DK = 128
GDN_DV = 128
GDN_W = GDN_HEADS * GDN_DK
GDN_CONV = 5
GDN_CHUNK = 64
XA_HEADS = 4
XA_DIM = 128
XA_W = XA_HEADS * XA_DIM
D_FF = 5632
FFN_CONV = 3
N_BRANCH = 3
EPS = 1e-6
IN_SPLITS = (DA_QK_W, DA_QK_W, DA_V_W, 3 * GDN_W, GDN_W, 4 * GDN_HEADS, XA_W, N_BRANCH * D_MODEL)
W_IN = DA_QK_W + DA_QK_W + DA_V_W + 3 * GDN_W + GDN_W + 4 * GDN_HEADS + XA_W + N_BRANCH * D_MODEL

kernel_name = "hybrid_diffattn_gdn_encoder"


def rmsnorm(x, g):
    xf = x.astype(jnp.float32)
    y = xf * lax.rsqrt(jnp.mean(xf * xf, axis=-1, keepdims=True) + EPS)
    return (y * g.astype(jnp.float32)).astype(x.dtype)


def l2norm(x):
    xf = x.astype(jnp.float32)
    return xf * lax.rsqrt(jnp.sum(xf * xf, axis=-1, keepdims=True) + EPS)


def centred_dwconv(x, w, b):
    K = w.shape[0]
    p = K // 2
    S = x.shape[1]
    xp = jnp.pad(x, ((0, 0), (p, p), (0, 0)))
    out = xp[:, 0:S] * w[0]
    for j in range(1, K):
        out = out + xp[:, j:j + S] * w[j]
    if b is not None:
        out = out + b
    return out


def alibi_slopes(n):
    return jnp.asarray([2.0 ** (-8.0 * (h + 1) / n) for h in range(n)], dtype=jnp.float32)


def diff_attention(q, k, v, lam, slopes):
    B, S = q.shape[0], q.shape[1]
    nb = S // Q_BLOCK
    scale = DA_DIM ** -0.5
    kpos = jnp.arange(S, dtype=jnp.float32)
    qb = q.reshape(B, nb, Q_BLOCK, DA_HEADS, 2, DA_DIM).transpose(1, 0, 2, 3, 4, 5)

    def block(args):
        qi, i = args
        s = jnp.einsum('bqhmd,bkhmd->bhmqk', qi, k, preferred_element_type=jnp.float32) * scale
        qpos = (i * Q_BLOCK + jnp.arange(Q_BLOCK)).astype(jnp.float32)
        bias = -slopes[:, None, None, None] * jnp.abs(qpos[:, None] - kpos[None, :])
        p = jax.nn.softmax(s + bias, axis=-1)
        wts = p[:, :, 0] - lam * p[:, :, 1]
        return jnp.einsum('bhqk,bkhe->bqhe', wts.astype(v.dtype), v)

    out = lax.map(block, (qb, jnp.arange(nb)))
    return out.transpose(1, 0, 2, 3, 4).reshape(B, S, DA_HEADS, DA_V)


def gated_delta_chunked(q, k, v, g, beta):
    B, S, H, DK = q.shape
    DV = v.shape[-1]
    C = GDN_CHUNK
    N = S // C
    q = q.reshape(B, N, C, H, DK).transpose(0, 1, 3, 2, 4)
    k = k.reshape(B, N, C, H, DK).transpose(0, 1, 3, 2, 4)
    v = v.reshape(B, N, C, H, DV).transpose(0, 1, 3, 2, 4)
    g = jnp.cumsum(g.reshape(B, N, C, H).transpose(0, 1, 3, 2), axis=-1)
    beta = beta.reshape(B, N, C, H).transpose(0, 1, 3, 2)
    causal = jnp.tril(jnp.ones((C, C), dtype=bool))
    strict = jnp.tril(jnp.ones((C, C), dtype=bool), -1)
    decay = jnp.exp(jnp.where(causal, g[..., :, None] - g[..., None, :], -jnp.inf))
    kb = k * beta[..., None]
    lmat = jnp.where(strict, jnp.einsum('bnhcd,bnhed->bnhce', kb, k) * decay, 0.0)
    amat = lmat + jnp.eye(C, dtype=jnp.float32)
    rhs = jnp.concatenate([v * beta[..., None], kb * jnp.exp(g)[..., None]], axis=-1)
    sol = lax.linalg.triangular_solve(amat, rhs, left_side=True, lower=True, unit_diagonal=True)
    u = sol[..., :DV]
    w = sol[..., DV:]
    qk = jnp.where(causal, jnp.einsum('bnhcd,bnhed->bnhce', q, k) * decay, 0.0)
    q_dec = q * jnp.exp(g)[..., None]
    g_last = g[..., -1]
    k_dec = k * jnp.exp(g_last[..., None] - g)[..., None]

    def step(state, xs):
        qk_i, qd_i, u_i, w_i, kd_i, gl_i = xs
        v_new = u_i - jnp.einsum('bhcd,bhde->bhce', w_i, state)
        o = jnp.einsum('bhcd,bhde->bhce', qd_i, state) + jnp.einsum('bhce,bhef->bhcf', qk_i, v_new)
        state = state * jnp.exp(gl_i)[..., None, None] + jnp.einsum('bhcd,bhce->bhde', kd_i, v_new)
        return state, o

    xs = tuple(jnp.moveaxis(t, 1, 0) for t in (qk, q_dec, u, w, k_dec, g_last))
    state0 = jnp.zeros((B, H, DK, DV), jnp.float32)
    _, o = lax.scan(step, state0, xs)
    return o.transpose(1, 0, 3, 2, 4).reshape(B, S, H, DV)


def memory_attention(q, mem_n, w_mem_kv, q_norm, k_norm):
    B, M = mem_n.shape[0], mem_n.shape[1]
    kv = mem_n @ w_mem_kv
    mk, mv = jnp.split(kv, 2, axis=-1)
    mk = rmsnorm(mk.reshape(B, M, XA_HEADS, XA_DIM), k_norm)
    mv = mv.reshape(B, M, XA_HEADS, XA_DIM)
    q = rmsnorm(q, q_norm)
    s = jnp.einsum('bshd,bmhd->bhsm', q, mk, preferred_element_type=jnp.float32) * (XA_DIM ** -0.5)
    p = jax.nn.softmax(s, axis=-1)
    o = jnp.einsum('bhsm,bmhd->bshd', p.astype(mv.dtype), mv)
    return o.reshape(q.shape[0], q.shape[1], XA_W)


def encoder_layer(x, mem, lambda_init, g_mix, g_mem, w_in, b_gate, da_q_norm, da_k_norm, da_lambda,
                  da_subln, gdn_conv_w, gdn_A_log, gdn_dt_bias, gdn_out_norm, xa_q_norm, xa_k_norm,
                  w_mem_kv, p_attn, p_gdn, p_mem, w_o, g_ffn, w_up, ffn_conv_w, ffn_conv_b, w_down):
    f32 = jnp.float32
    B, S, _ = x.shape
    h = rmsnorm(x, g_mix)
    proj = h @ w_in
    idx = [int(i) for i in np.cumsum(IN_SPLITS)[:-1]]
    da_q, da_k, da_v, gdn_qkv, gdn_z, gdn_ab, xa_q, gate_logit = jnp.split(proj, idx, axis=-1)
    gates = jax.nn.sigmoid((gate_logit + b_gate).astype(f32)).astype(x.dtype)
    gates = gates.reshape(B, S, N_BRANCH, D_MODEL)

    q = rmsnorm(da_q.reshape(B, S, DA_HEADS, 2, DA_DIM), da_q_norm)
    k = rmsnorm(da_k.reshape(B, S, DA_HEADS, 2, DA_DIM), da_k_norm)
    v = da_v.reshape(B, S, DA_HEADS, DA_V)
    lp = da_lambda.astype(f32)
    lam = jnp.exp(jnp.sum(lp[0] * lp[1])) - jnp.exp(jnp.sum(lp[2] * lp[3])) + lambda_init
    o = diff_attention(q, k, v, lam, alibi_slopes(DA_HEADS))
    a_out = (rmsnorm(o, da_subln) * (1.0 - lambda_init)).reshape(B, S, DA_V_W)

    qkv = jax.nn.silu(centred_dwconv(gdn_qkv, gdn_conv_w, None))
    gq, gk, gv = jnp.split(qkv, 3, axis=-1)
    gq = l2norm(gq.reshape(B, S, GDN_HEADS, GDN_DK)) * (GDN_DK ** -0.5)
    gk = l2norm(gk.reshape(B, S, GDN_HEADS, GDN_DK))
    gv = gv.reshape(B, S, GDN_HEADS, GDN_DV).astype(f32)
    ab = gdn_ab.astype(f32).reshape(B, S, 4, GDN_HEADS)
    beta = jax.nn.sigmoid(ab[:, :, 0:2])
    g = -jnp.exp(gdn_A_log.astype(f32)) * jax.nn.softplus(ab[:, :, 2:4] + gdn_dt_bias.astype(f32))
    o_f = gated_delta_chunked(gq, gk, gv, g[:, :, 0], beta[:, :, 0])
    o_b = jnp.flip(gated_delta_chunked(jnp.flip(gq, 1), jnp.flip(gk, 1), jnp.flip(gv, 1),
                                       jnp.flip(g[:, :, 1], 1), jnp.flip(beta[:, :, 1], 1)), 1)
    od = rmsnorm(o_f + o_b, gdn_out_norm) * jax.nn.silu(gdn_z.reshape(B, S, GDN_HEADS, GDN_DV).astype(f32))
    d_out = od.reshape(B, S, GDN_W).astype(x.dtype)

    c_out = memory_attention(xa_q.reshape(B, S, XA_HEADS, XA_DIM), rmsnorm(mem, g_mem), w_mem_kv,
                             xa_q_norm, xa_k_norm)

    merged = (gates[:, :, 0] * (a_out @ p_attn) + gates[:, :, 1] * (d_out @ p_gdn)
              + gates[:, :, 2] * (c_out @ p_mem))
    x = x + merged @ w_o

    h2 = rmsnorm(x, g_ffn)
    gt, up = jnp.split(h2 @ w_up, 2, axis=-1)
    gt = centred_dwconv(gt, ffn_conv_w, ffn_conv_b)
    x = x + (jax.nn.silu(gt) * up) @ w_down
    return x


def run_trunk(x, mem, g_mix, g_mem, w_in, b_gate, da_q_norm, da_k_norm, da_lambda, da_subln,
              gdn_conv_w, gdn_A_log, gdn_dt_bias, gdn_out_norm, xa_q_norm, xa_k_norm, w_mem_kv,
              p_attn, p_gdn, p_mem, w_o, g_ffn, w_up, ffn_conv_w, ffn_conv_b, w_down):
    for l in range(DEPTH):
        lambda_init = 0.8 - 0.6 * math.exp(-0.3 * l)
        x = encoder_layer(x, mem, lambda_init, g_mix[l], g_mem[l], w_in[l], b_gate[l], da_q_norm[l],
                          da_k_norm[l], da_lambda[l], da_subln[l], gdn_conv_w[l], gdn_A_log[l],
                          gdn_dt_bias[l], gdn_out_norm[l], xa_q_norm[l], xa_k_norm[l], w_mem_kv[l],
                          p_attn[l], p_gdn[l], p_mem[l], w_o[l], g_ffn[l], w_up[l], ffn_conv_w[l],
                          ffn_conv_b[l], w_down[l])
    return x


def setup_inputs(seed: int = 0) -> dict:
    key = jax.random.key(seed)
    ks = jax.random.split(key, 32)
    f32 = jnp.float32
    L = DEPTH

    def nrm(k, shape, scale):
        return jax.random.normal(k, shape, f32) * scale

    def gain(k, shape):
        return 1.0 + 0.01 * jax.random.normal(k, shape, f32)

    dt = jnp.exp(jax.random.uniform(ks[10], (L, 2, GDN_HEADS), f32, math.log(1e-3), math.log(1e-1)))
    return {
        'x_prompt': nrm(ks[0], (BATCH, SEQ, D_MODEL), 1.0),
        'x_sample': nrm(ks[1], (DEC_BATCH, DEC_SEQ, D_MODEL), 1.0),
        'mem_prompt': nrm(ks[2], (BATCH, N_MEM, D_MODEL), 1.0),
        'mem_sample': nrm(ks[3], (DEC_BATCH, N_MEM, D_MODEL), 1.0),
        'g_mix': gain(ks[4], (L, D_MODEL)),
        'g_mem': gain(ks[5], (L, D_MODEL)),
        'w_in': nrm(ks[6], (L, D_MODEL, W_IN), D_MODEL ** -0.5),
        'b_gate': nrm(ks[7], (L, N_BRANCH * D_MODEL), 0.01),
        'da_q_norm': gain(ks[8], (L, DA_DIM)),
        'da_k_norm': gain(ks[9], (L, DA_DIM)),
        'da_lambda': nrm(ks[11], (L, 4, DA_DIM), 0.1),
        'da_subln': gain(ks[12], (L, DA_V)),
        'gdn_conv_w': nrm(ks[13], (L, GDN_CONV, 3 * GDN_W), GDN_CONV ** -0.5),
        'gdn_A_log': jnp.log(jax.random.uniform(ks[14], (L, 2, GDN_HEADS), f32, 1.0, 16.0)),
        'gdn_dt_bias': dt + jnp.log(-jnp.expm1(-dt)),
        'gdn_out_norm': gain(ks[15], (L, GDN_DV)),
        'xa_q_norm': gain(ks[16], (L, XA_DIM)),
        'xa_k_norm': gain(ks[17], (L, XA_DIM)),
        'w_mem_kv': nrm(ks[18], (L, D_MODEL, 2 * XA_W), D_MODEL ** -0.5),
        'p_attn': nrm(ks[19], (L, DA_V_W, D_MODEL), DA_V_W ** -0.5),
        'p_gdn': nrm(ks[20], (L, GDN_W, D_MODEL), GDN_W ** -0.5),
        'p_mem': nrm(ks[21], (L, XA_W, D_MODEL), XA_W ** -0.5),
        'w_o': nrm(ks[22], (L, D_MODEL, D_MODEL), D_MODEL ** -0.5),
        'g_ffn': gain(ks[23], (L, D_MODEL)),
        'w_up': nrm(ks[24], (L, D_MODEL, 2 * D_FF), D_MODEL ** -0.5),
        'ffn_conv_w': nrm(ks[25], (L, FFN_CONV, D_FF), FFN_CONV ** -0.5),
        'ffn_conv_b': nrm(ks[26], (L, D_FF), 0.01),
        'w_down': nrm(ks[27], (L, D_FF, D_MODEL), D_FF ** -0.5),
    }


def reference(x_prompt, x_sample, mem_prompt, mem_sample, g_mix, g_mem, w_in, b_gate, da_q_norm,
              da_k_norm, da_lambda, da_subln, gdn_conv_w, gdn_A_log, gdn_dt_bias, gdn_out_norm,
              xa_q_norm, xa_k_norm, w_mem_kv, p_attn, p_gdn, p_mem, w_o, g_ffn, w_up, ffn_conv_w,
              ffn_conv_b, w_down):
    y_prompt = run_trunk(x_prompt, mem_prompt, g_mix, g_mem, w_in, b_gate, da_q_norm, da_k_norm,
                         da_lambda, da_subln, gdn_conv_w, gdn_A_log, gdn_dt_bias, gdn_out_norm,
                         xa_q_norm, xa_k_norm, w_mem_kv, p_attn, p_gdn, p_mem, w_o, g_ffn, w_up,
                         ffn_conv_w, ffn_conv_b, w_down)
    y_sample = run_trunk(x_sample, mem_sample, g_mix, g_mem, w_in, b_gate, da_q_norm, da_k_norm,
                         da_lambda, da_subln, gdn_conv_w, gdn_A_log, gdn_dt_bias, gdn_out_norm,
                         xa_q_norm, xa_k_norm, w_mem_kv, p_attn, p_gdn, p_mem, w_o, g_ffn, w_up,
                         ffn_conv_w, ffn_conv_b, w_down)
    return (y_prompt, y_sample)
```

```python
import math
import numpy as np
import concourse.bass as bass
import concourse.mybir as mybir
from concourse.bass_utils import run_bass_kernel_spmd
from contextlib import ExitStack

F32 = mybir.dt.float32
BF16 = mybir.dt.bfloat16
AF = mybir.ActivationFunctionType
ALU = mybir.AluOpType
AX = mybir.AxisListType

D = 2048
W_IN = 21056
DFF = 5632
EPS = 1e-6
NEG = -30000.0
SLOPES = [2.0 ** (-(h + 1)) for h in range(8)]
LAMBDA_INIT = 0.8 - 0.6 * math.exp(0.0)


class Buf:
    __slots__ = ("name", "w", "rs")

    def __init__(self, name):
        self.name = name
        self.w = None
        self.rs = []


class Op:
    __slots__ = ("eng", "fn", "dma", "deps", "need_inc", "sem", "val", "stream")


class Sched:
    def __init__(self, nc, es):
        self.nc = nc
        self.es = es
        self.ops = []
        self.eng = {"pe": nc.tensor, "act": nc.scalar, "dve": nc.vector, "pool": nc.gpsimd, "sp": nc.sync}
        self.last = {}
        self.pending_barrier = {}

    def op(self, eng, fn, reads=(), writes=(), dma=False, stream=None):
        o = Op()
        o.eng = eng
        o.fn = fn
        o.dma = dma
        o.need_inc = dma
        o.sem = None
        o.val = 0
        o.stream = stream
        deps = set()
        for b in reads:
            if b.w is not None:
                deps.add(b.w)
        for b in writes:
            if b.w is not None:
                deps.add(b.w)
            deps.update(b.rs)
        pb = self.pending_barrier.pop(eng, None)
        if pb:
            deps.update(pb)
        for b in reads:
            b.rs.append(o)
        for b in writes:
            b.w = o
            b.rs = []
        deps.discard(o)
        o.deps = deps
        self.ops.append(o)
        if dma:
            self.last[("dma", id(stream))] = o
        else:
            self.last[eng] = o
        return o

    def barrier(self):
        tails = set(self.last.values())
        for e in self.eng:
            s = self.pending_barrier.setdefault(e, set())
            s.update(tails)

    @staticmethod
    def _needs_wait(o, d):
        if (not o.dma) and (not d.dma) and o.eng == d.eng and o.eng == "pe":
            return False
        return True

    def emit(self):
        nc, es = self.nc, self.es
        for o in self.ops:
            for d in o.deps:
                if self._needs_wait(o, d):
                    d.need_inc = True
        engsem = {e: es.enter_context(nc.semaphore("sem_" + e)) for e in self.eng}
        cnt = {e: 0 for e in self.eng}
        streams = {}
        for o in self.ops:
            if o.dma:
                k = id(o.stream)
                if k not in streams:
                    streams[k] = [es.enter_context(nc.semaphore("dsem%d" % len(streams))), 0]
                st = streams[k]
                st[1] += 16
                o.sem, o.val = st[0], st[1]
            elif o.need_inc:
                cnt[o.eng] += 1
                o.sem, o.val = engsem[o.eng], cnt[o.eng]
        self.n_sems = len(streams) + len(engsem)
        waited = {e: {} for e in self.eng}
        nwait = 0
        for o in self.ops:
            e = self.eng[o.eng]
            w = {}
            for d in o.deps:
                if self._needs_wait(o, d):
                    k = id(d.sem)
                    if k not in w or w[k][1] < d.val:
                        w[k] = (d.sem, d.val)
            wd = waited[o.eng]
            for k, (sem, val) in w.items():
                if wd.get(k, 0) < val:
                    e.wait_ge(sem, val)
                    wd[k] = val
                    nwait += 1
            ins = o.fn()
            if o.need_inc:
                ins.then_inc(o.sem, 16 if o.dma else 1)
        sp = self.eng["sp"]
        for k, (sem, val) in streams.items():
            if waited["sp"].get(id(sem), 0) < val:
                sp.wait_ge(sem, val)
        self.nwait = nwait


class Tl:
    __slots__ = ("t", "b")

    def __init__(self, t, name):
        self.t = t
        self.b = Buf(name)


def build(SH, debug=False, do_gdn=True):
    T = 2 * SH
    NT = T // 128
    NG = T // 512
    NGH = SH // 512
    NKC = T // 128
    TB = min(2048, T)
    NTB = T // TB
    nc = bass.Bass("TRN2", target_bir_lowering=False)
    es = ExitStack()
    S = Sched(nc, es)

    def din(name, shape, dt=F32):
        return nc.dram_tensor(name, list(shape), dt, kind="ExternalInput").ap()

    def dscr(name, shape, dt):
        return nc.dram_tensor(name, list(shape), dt, kind="ExternalOutput" if debug else "Internal").ap()

    x_d = din("x", [T, D])
    mem_d = din("mem", [2, 256, D])
    w_in_d = din("w_in", [D, W_IN])
    w_kv_d = din("w_mem_kv", [D, 1024])
    p_attn_d = din("p_attn", [D, D])
    p_gdn_d = din("p_gdn", [D, D])
    p_mem_d = din("p_mem", [512, D])
    w_o_d = din("w_o", [D, D])
    w_up_d = din("w_up", [D, 2 * DFF])
    w_dn_d = din("w_down", [DFF, D])
    NCV = 480
    colv_d = din("colv", [128, NCV])
    rowv_d = din("rowv", [8, D])
    cb_d = din("cb", [8, 128, NG * NKC])
    flag_d = din("flag", [128, 2])
    y_d = nc.dram_tensor("y", [T, D], F32, kind="ExternalOutput").ap()

    qnT_s = dscr("qnT_s", [16, 128, T], BF16)
    knT_s = dscr("knT_s", [16, 128, T], BF16)
    v_s = dscr("v_s", [T, D], BF16)
    qkvT_s = dscr("qkvT_s", [48, 128, T], BF16)
    z_s = dscr("z_s", [T, D], BF16)
    ab_s = dscr("ab_s", [T, 64], F32)
    xqnT_s = dscr("xqnT_s", [4, 128, T], BF16)
    gT_s = dscr("gT_s", [48, 128, T], BF16)
    aT_s = dscr("aT_s", [16, 128, T], BF16)
    dT_s = dscr("dT_s", [16, 128, T], BF16)
    cT_s = dscr("cT_s", [4, 128, T], BF16)
    h2T_s = dscr("h2T_s", [16, 128, T], BF16)
    x1_s = dscr("x1_s", [T, D], F32)

    CV = {}
    _c = [0]

    def cv_alloc(name, n):
        CV[name] = _c[0]
        _c[0] += n
    cv_alloc("da_q_norm", 1); cv_alloc("da_k_norm", 1); cv_alloc("xa_q_norm", 1); cv_alloc("xa_k_norm", 1)
    cv_alloc("da_subln", 2); cv_alloc("da_lambda", 4); cv_alloc("b_gate", 48); cv_alloc("ffn_b", 44)
    cv_alloc("ffn_w", 132); cv_alloc("gdn_w", 240)
    assert _c[0] <= NCV

    es_ph = [None]
    uid = [0]

    def sb(name, shape, dt):
        uid[0] += 1
        name = "sb%d_%s" % (uid[0], name)
        return Tl(es_ph[0].enter_context(nc.sbuf_tensor(name, list(shape), dt)), name)

    def ps(name, shape, dt=F32):
        uid[0] += 1
        name = "ps%d_%s" % (uid[0], name)
        return Tl(es_ph[0].enter_context(nc.psum_tensor(name, list(shape), dt)), name)

    def dbuf(name):
        uid[0] += 1
        return Buf(name + str(uid[0]))

    def DMA(eng, out_ap, in_ap, reads, writes, stream):
        e = S.eng[eng]
        return S.op(eng, lambda: e.dma_start(out=out_ap, in_=in_ap), reads, writes, dma=True, stream=stream)

    def MM(out_ap, lhsT, rhs, start, stop, reads, writes):
        return S.op("pe", lambda: nc.tensor.matmul(out_ap, lhsT, rhs, start=start, stop=stop), reads, writes)

    def TR(out_ap, in_ap, ident_ap, reads, writes):
        return S.op("pe", lambda: nc.tensor.transpose(out_ap, in_ap, ident_ap), reads, writes)

    def ACT(out_ap, in_ap, func, reads, writes, bias=None, scale=None, accum_out=None):
        kw = {}
        if bias is not None:
            kw["bias"] = bias
        if scale is not None:
            kw["scale"] = scale
        if accum_out is not None:
            kw["accum_out"] = accum_out
        return S.op("act", lambda: nc.scalar.activation(out=out_ap, in_=in_ap, func=func, **kw), reads, writes)

    def veng(eng):
        return nc.vector if eng == "dve" else nc.gpsimd

    def TT(eng, out_ap, in0, in1, op, reads, writes):
        e = veng(eng)
        return S.op(eng, lambda: e.tensor_tensor(out=out_ap, in0=in0, in1=in1, op=op), reads, writes)

    def TS(eng, out_ap, in0, s1, s2, op0, op1, reads, writes):
        e = veng(eng)
        if op1 is None:
            return S.op(eng, lambda: e.tensor_scalar(out=out_ap, in0=in0, scalar1=s1, scalar2=None, op0=op0), reads, writes)
        return S.op(eng, lambda: e.tensor_scalar(out=out_ap, in0=in0, scalar1=s1, scalar2=s2, op0=op0, op1=op1), reads, writes)

    def STT(eng, out_ap, in0, scalar, in1, op0, op1, reads, writes):
        e = veng(eng)
        return S.op(eng, lambda: e.scalar_tensor_tensor(out=out_ap, in0=in0, scalar=scalar, in1=in1, op0=op0, op1=op1), reads, writes)

    def CP(eng, out_ap, in_ap, reads, writes):
        if eng == "act":
            return S.op("act", lambda: nc.scalar.copy(out=out_ap, in_=in_ap), reads, writes)
        e = veng(eng)
        return S.op(eng, lambda: e.tensor_copy(out=out_ap, in_=in_ap), reads, writes)

    def MSET(eng, ap, val, writes):
        e = veng(eng)
        return S.op(eng, lambda: e.memset(ap, val), (), writes)

    def rsqrt_mean(eng, out_ap, in_ap, n, reads, writes):
        TS(eng, out_ap, in_ap, 1.0 / n, EPS, ALU.mult, ALU.add, reads, writes)
        ACT(out_ap, out_ap, AF.Sqrt, writes, writes)
        S.op("dve", lambda: nc.vector.reciprocal(out=out_ap, in_=out_ap), writes, writes)

    es_ph[0] = es
    ident_bf = sb("ident_bf", [128, 128], BF16)
    ident_f = sb("ident_f", [128, 128], F32)
    ones_bf = sb("ones_bf", [128, 128], BF16)
    ones_f = sb("ones_f", [128, 128], F32)
    colv = sb("colv", [128, NCV], F32)
    flag = sb("flag", [128, 2], F32)
    iota_p = sb("iota_p", [128, 1], F32)
    iota_f = sb("iota_f", [128, 512], F32)
    S.op("pool", lambda: nc.gpsimd.iota(iota_p.t[:], pattern=[[0, 1]], base=0, channel_multiplier=1,
                                         allow_small_or_imprecise_dtypes=True), (), [iota_p.b])
    S.op("pool", lambda: nc.gpsimd.iota(iota_f.t[:], pattern=[[1, 512]], base=0, channel_multiplier=0,
                                         allow_small_or_imprecise_dtypes=True), (), [iota_f.b])
    TS("dve", ident_f.t[:], iota_f.t[:, 0:128], iota_p.t[:, 0:1], None, ALU.is_equal, None, [iota_f.b, iota_p.b], [ident_f.b])
    CP("dve", ident_bf.t[:], ident_f.t[:], [ident_f.b], [ident_bf.b])
    MSET("dve", ones_f.t[:], 1.0, [ones_f.b])
    MSET("dve", ones_bf.t[:], 1.0, [ones_bf.b])
    DMA("sp", colv.t[:], colv_d[:, :], (), [colv.b], colv)
    DMA("sp", flag.t[:], flag_d[:, :], (), [flag.b], flag)

    def cvc(name, i=0):
        c = CV[name] + i
        return colv.t[:, c:c + 1]

    def fm_norm_epilogue(pst, n, gain_ap, out_tl, sq_tl, ss_ps, r_tl, ndim=128):
        ACT(sq_tl.t[:, :n], pst.t[:, :n], AF.Square, [pst.b], [sq_tl.b])

        def rest():
            MM(ss_ps.t[:, :n], ones_f.t[:], sq_tl.t[:, :n], True, True, [ones_f.b, sq_tl.b], [ss_ps.b])
            rsqrt_mean("dve", r_tl.t[:, :n], ss_ps.t[:, :n], ndim, [ss_ps.b], [r_tl.b])
            STT("dve", out_tl.t[:, :n], pst.t[:, :n], gain_ap, r_tl.t[:, :n], ALU.mult, ALU.mult,
                [pst.b, colv.b, r_tl.b], [out_tl.b])
        return rest

    segs = []
    for i in range(4): segs.append((0 + 512 * i, 512, "q", i))
    for i in range(4): segs.append((2048 + 512 * i, 512, "k", i))
    for i in range(4): segs.append((4096 + 512 * i, 512, "v", i))
    for i in range(12): segs.append((6144 + 512 * i, 512, "gqkv", i))
    for i in range(4): segs.append((12288 + 512 * i, 512, "z", i))
    segs.append((14336, 64, "ab", 0))
    segs.append((14400, 512, "xq", 0))
    for i in range(12): segs.append((14912 + 512 * i, 512, "gate", i))

    with ExitStack() as ph:
        es_ph[0] = ph
        gmix_bc = sb("gmix_bc", [128, D], F32)
        DMA("sp", gmix_bc.t[:], rowv_d[0:1, :].partition_broadcast(128), (), [gmix_bc.b], gmix_bc)
        xt = [sb("xt%d" % i, [128, D], F32) for i in range(2)]
        junk = sb("junk", [128, D], BF16)
        ssq = [sb("ssq%d" % i, [128, 1], F32) for i in range(2)]
        hn = [sb("hn%d" % i, [128, D], BF16) for i in range(2)]
        hT = sb("hT", [128, 16, TB], BF16)
        wch = [sb("wch%d" % i, [128, 16, 512], BF16) for i in range(2)]
        stage = [sb("stage%d" % i, [128, 512], F32) for i in range(4)]
        stage_bf = [sb("stagebf%d" % i, [128, 512], BF16) for i in range(4)]
        sqb = [sb("sqb%d" % i, [128, 512], F32) for i in range(2)]
        rb = [sb("rb%d" % i, [128, 512], F32) for i in range(2)]
        tp_ps = [ps("tp_ps%d" % i, [128, 16, 128], BF16) for i in range(1)]
        mm_ps = [ps("mm_ps%d" % i, [128, 512], F32) for i in range(4)]
        ss_ps = [ps("ss_ps%d" % i, [128, 512], F32) for i in range(2)]
        cnt = {"st": 0, "mm": 0, "ss": 0, "w": 0}
        deferred = []

        def nxt(key, lst):
            i = cnt[key] % len(lst)
            cnt[key] += 1
            return lst[i], i

        for tb in range(NTB):
            for ti in range(TB // 128):
                r0 = tb * TB + ti * 128
                X = xt[ti % 2]
                DMA("sp", X.t[:], x_d[r0:r0 + 128, :], (), [X.b], X)
                sq_ = ssq[ti % 2]
                ACT(junk.t[:], X.t[:], AF.Square, [X.b], [junk.b, sq_.b], accum_out=sq_.t[:, 0:1])
                rsqrt_mean("dve", sq_.t[:, 0:1], sq_.t[:, 0:1], D, [sq_.b], [sq_.b])
                H = hn[ti % 2]
                STT("dve", H.t[:], X.t[:], sq_.t[:, 0:1], gmix_bc.t[:], ALU.mult, ALU.mult, [X.b, sq_.b, gmix_bc.b], [H.b])
                tp = tp_ps[0]
                for k in range(16):
                    TR(tp.t[:, k, :], H.t[:, k * 128:(k + 1) * 128], ident_bf.t[:], [H.b, ident_bf.b], [tp.b])
                CP("act", hT.t[:, :, ti * 128:(ti + 1) * 128], tp.t[:], [tp.b], [hT.b])
            for (c0, ncol, kind, idx) in segs:
                W, _ = nxt("w", wch)
                DMA("pool", W.t[:, :, :ncol], w_in_d[:, c0:c0 + ncol].rearrange("(k p) c -> p k c", p=128),
                    (), [W.b], W)
                if kind in ("q", "k", "gqkv", "xq", "gate"):
                    for cbk in range(ncol // 128):
                        for tg in range(TB // 512):
                            t0 = tb * TB + tg * 512
                            P, _ = nxt("mm", mm_ps)
                            for k in range(16):
                                MM(P.t[:], W.t[:, k, cbk * 128:(cbk + 1) * 128], hT.t[:, k, tg * 512:(tg + 1) * 512],
                                   k == 0, k == 15, [W.b, hT.b], [P.b])
                            while deferred:
                                deferred.pop(0)()
                            ob, si = nxt("st", stage_bf)
                            blk = idx * 4 + cbk
                            if kind in ("q", "k", "xq"):
                                gname = {"q": "da_q_norm", "k": "da_k_norm", "xq": "xa_q_norm"}[kind]
                                SS, j = nxt("ss", ss_ps)
                                rest = fm_norm_epilogue(P, 512, cvc(gname), ob, sqb[j], SS, rb[j])
                                dst = {"q": qnT_s, "k": knT_s, "xq": xqnT_s}[kind]

                                def fin(rest=rest, dst=dst, blk=blk, t0=t0, ob=ob):
                                    rest()
                                    DMA("sp", dst[blk, :, t0:t0 + 512], ob.t[:], [ob.b], [dbuf("d")], ob)
                                deferred.append(fin)
                                continue
                            elif kind == "gqkv":
                                CP("act", ob.t[:], P.t[:], [P.b], [ob.b])
                                dst = qkvT_s
                            else:
                                ACT(ob.t[:], P.t[:], AF.Sigmoid, [P.b, colv.b], [ob.b], bias=cvc("b_gate", blk))
                                dst = gT_s
                            DMA("sp", dst[blk, :, t0:t0 + 512], ob.t[:], [ob.b], [dbuf("d")], ob)
                else:
                    while deferred:
                        deferred.pop(0)()
                    for tt in range(TB // 128):
                        t0 = tb * TB + tt * 128
                        P, _ = nxt("mm", mm_ps)
                        for k in range(16):
                            MM(P.t[:, :ncol], hT.t[:, k, tt * 128:(tt + 1) * 128], W.t[:, k, :ncol],
                               k == 0, k == 15, [W.b, hT.b], [P.b])
                        if kind == "ab":
                            ob, si = nxt("st", stage)
                            CP("act", ob.t[:, :64], P.t[:, :64], [P.b], [ob.b])
                            DMA("sp", ab_s[t0:t0 + 128, :], ob.t[:, :64], [ob.b], [dbuf("d")], ob)
                        else:
                            ob, si = nxt("st", stage_bf)
                            if kind == "v":
                                CP("act", ob.t[:], P.t[:], [P.b], [ob.b])
                                dst = v_s
                            else:
                                ACT(ob.t[:], P.t[:], AF.Silu, [P.b], [ob.b])
                                dst = z_s
                            DMA("sp", dst[t0:t0 + 128, idx * 512:(idx + 1) * 512], ob.t[:], [ob.b], [dbuf("d")], ob)
            while deferred:
                deferred.pop(0)()
        S.barrier()
    es_ph[0] = es

    WST = object()
    wbufs = []
    wbf = {}
    for nm_, src_, rows_, cols_ in (("p_attn", p_attn_d, D, D), ("p_gdn", p_gdn_d, D, D), ("p_mem", p_mem_d, 512, D),
                                    ("w_o", w_o_d, D, D), ("w_up", w_up_d, D, 2 * DFF), ("w_down", w_dn_d, DFF, D)):
        dst_ = nc.dram_tensor("wbf_" + nm_, [rows_, cols_], BF16, kind="Internal").ap()
        wbf[nm_] = dst_
        for r_ in range(0, rows_, 512):
            r1_ = min(rows_, r_ + 512)
            b_ = dbuf("w")
            wbufs.append(b_)
            DMA("pool", dst_[r_:r1_, :], src_[r_:r1_, :], [], [b_], WST)
    p_attn_d, p_gdn_d, p_mem_d, w_o_d, w_up_d, w_dn_d = (wbf["p_attn"], wbf["p_gdn"], wbf["p_mem"], wbf["w_o"],
                                                         wbf["w_up"], wbf["w_down"])

    scale = 128.0 ** -0.5
    with ExitStack() as ph:
        es_ph[0] = ph
        Bp = sb("Bp", [128, 512], F32)
        Bd = [sb("Bd%d" % c, [128, 512], F32) for c in range(4)]
        TS("dve", Bp.t[:], iota_f.t[:], iota_p.t[:, 0:1], None, ALU.subtract, None, [iota_f.b, iota_p.b], [Bp.b])
        negb = sb("negb", [128, 512], F32)
        for c in range(4):
            TS("dve", Bd[c].t[:], Bp.t[:], float(-128 * c), None, ALU.add, None, [Bp.b], [Bd[c].b])
            TS("dve", negb.t[:], Bd[c].t[:], -1.0, None, ALU.mult, None, [Bd[c].b], [negb.b])
            TT("dve", Bd[c].t[:], Bd[c].t[:], negb.t[:], ALU.max, [Bd[c].b, negb.b], [Bd[c].b])
        lam_t = sb("lam_t", [128, 4], F32)
        neglam = sb("neglam", [128, 1], F32)
        subg = sb("subg", [128, 2], F32)
        NSL, LA = 5, 4
        s_ps = [ps("s_ps%d" % i, [128, 512], F32) for i in range(NSL)]
        lam_ps = s_ps[0]
        c_l = CV["da_lambda"]
        TT("dve", lam_t.t[:, 0:1], colv.t[:, c_l:c_l + 1], colv.t[:, c_l + 1:c_l + 2], ALU.mult, [colv.b], [lam_t.b])
        TT("dve", lam_t.t[:, 1:2], colv.t[:, c_l + 2:c_l + 3], colv.t[:, c_l + 3:c_l + 4], ALU.mult, [colv.b, lam_t.b], [lam_t.b])
        MM(lam_ps.t[:, 0:2], ones_f.t[:], lam_t.t[:, 0:2], True, True, [ones_f.b, lam_t.b], [lam_ps.b])
        ACT(lam_t.t[:, 2:4], lam_ps.t[:, 0:2], AF.Exp, [lam_ps.b, lam_t.b], [lam_t.b])
        STT("dve", neglam.t[:], lam_t.t[:, 3:4], -LAMBDA_INIT, lam_t.t[:, 2:3], ALU.add, ALU.subtract, [lam_t.b], [neglam.b])
        c_s = CV["da_subln"]
        TS("dve", subg.t[:], colv.t[:, c_s:c_s + 2], 1.0 - LAMBDA_INIT, None, ALU.mult, None, [colv.b], [subg.b])

        qT = [sb("qT%d" % m, [128, T], BF16) for m in range(2)]
        kT = [sb("kT%d" % m, [128, T], BF16) for m in range(2)]
        Vh = sb("Vh", [128, NKC, 256], BF16)
        cbh = sb("cbh", [128, NG * NKC], F32)
        tmpb = [sb("tmpb%d" % i, [128, 512], F32) for i in range(4)]
        Pb = [sb("Pb%d" % i, [128, 512], BF16) for i in range(4)]
        o_ps = [[ps("o_ps%d_%d" % (i, j), [128, 512], F32) for j in range(3)] for i in range(1)]
        scnt = [0]
        sq_slots = []
        rl = sb("rl", [128, 512], F32)
        lacc = [sb("lacc%d" % i, [128, 512], F32) for i in range(2)]
        om = [[sb("om%d_%d" % (m, j), [128, 512], F32) for j in range(2)] for m in range(2)]
        dd = [sb("dd%d" % j, [128, 512], F32) for j in range(2)]
        sq2 = [sb("sq2_%d" % j, [128, 512], F32) for j in range(2)]
        r2 = sb("r2", [128, 512], F32)
        ao = [sb("ao%d" % j, [128, 512], BF16) for j in range(4)]
        it = 0
        oset = 0
        aoc = 0
        for h in range(8):
            DMA("sp", cbh.t[:], cb_d[h, :, :], [], [cbh.b], cbh)
            DMA("sp", qT[0].t[:], qnT_s[2 * h, :, :], [], [qT[0].b], qT[0])
            DMA("sp", kT[0].t[:], knT_s[2 * h, :, :], [], [kT[0].b], kT[0])
            DMA("sp", Vh.t[:], v_s[:, h * 256:(h + 1) * 256].rearrange("(kc p) c -> p kc c", p=128), [], [Vh.b], Vh)
            DMA("sp", qT[1].t[:], qnT_s[2 * h + 1, :, :], [], [qT[1].b], qT[1])
            DMA("sp", kT[1].t[:], knT_s[2 * h + 1, :, :], [], [kT[1].b], kT[1])
            sl = SLOPES[h]
            for g in range(NG):
                for m in range(2):
                    OS = o_ps[0]
                    oset += 1
                    def emit_S(kc_):
                        SPx = s_ps[scnt[0] % NSL]
                        scnt[0] += 1
                        MM(SPx.t[:], kT[m].t[:, kc_ * 128:(kc_ + 1) * 128], qT[m].t[:, g * 512:(g + 1) * 512], True, True,
                           [kT[m].b, qT[m].b], [SPx.b])
                        sq_slots.append(SPx)
                    act_kc = []
                    for kc_ in range(NKC):
                        q_lo, q_hi = g * 512, g * 512 + 511
                        k_lo, k_hi = kc_ * 128, kc_ * 128 + 127
                        gap = max(0, k_lo - q_hi, q_lo - k_hi)
                        if sl * gap < 150.0:
                            act_kc.append(kc_)
                    n_act = len(act_kc)
                    for j_ in range(min(LA, n_act)):
                        emit_S(act_kc[j_])
                    for ai, kc in enumerate(act_kc):
                        SP_ = sq_slots.pop(0)
                        TM = tmpb[it % 4]
                        PP = Pb[it % 4]
                        it += 1
                        if ai + LA < n_act:
                            emit_S(act_kc[ai + LA])
                        dlt = g * 512 - kc * 128
                        if dlt >= 127:
                            base, mult = Bp, -sl / scale
                        elif dlt <= -511:
                            base, mult = Bp, sl / scale
                        else:
                            base, mult = Bd[(-dlt) // 128], -sl / scale
                        STT("dve", TM.t[:], base.t[:], mult, SP_.t[:], ALU.mult, ALU.add, [base.b, SP_.b], [TM.b])
                        ci = g * NKC + kc
                        ACT(PP.t[:], TM.t[:], AF.Exp, [TM.b, cbh.b], [PP.b], bias=cbh.t[:, ci:ci + 1], scale=scale)
                        MM(OS[0].t[:], Vh.t[:, kc, 0:128], PP.t[:], ai == 0, ai == n_act - 1, [Vh.b, PP.b], [OS[0].b])
                        MM(OS[1].t[:], Vh.t[:, kc, 128:256], PP.t[:], ai == 0, ai == n_act - 1, [Vh.b, PP.b], [OS[1].b])
                        LAx = lacc[ai % 2]
                        if ai < 2:
                            CP("pool", LAx.t[:], PP.t[:], [PP.b], [LAx.b])
                        else:
                            TT("pool", LAx.t[:], LAx.t[:], PP.t[:], ALU.add, [LAx.b, PP.b], [LAx.b])
                    MM(OS[2].t[:], ones_f.t[:], lacc[0].t[:], True, False, [ones_f.b, lacc[0].b], [OS[2].b])
                    MM(OS[2].t[:], ones_f.t[:], lacc[1].t[:], False, True, [ones_f.b, lacc[1].b], [OS[2].b])
                    S.op("dve", lambda rl=rl, OS=OS: nc.vector.reciprocal(out=rl.t[:], in_=OS[2].t[:]), [OS[2].b], [rl.b])
                    for j in range(2):
                        TT("dve", om[m][j].t[:], OS[j].t[:], rl.t[:], ALU.mult, [OS[j].b, rl.b], [om[m][j].b])
                for j in range(2):
                    STT("dve", dd[j].t[:], om[1][j].t[:], neglam.t[:, 0:1], om[0][j].t[:], ALU.mult, ALU.add,
                        [om[1][j].b, om[0][j].b, neglam.b], [dd[j].b])
                    ACT(sq2[j].t[:], dd[j].t[:], AF.Square, [dd[j].b], [sq2[j].b])
                SSP = s_ps[scnt[0] % NSL]
                scnt[0] += 1
                for j in range(2):
                    MM(SSP.t[:], ones_f.t[:], sq2[j].t[:], j == 0, j == 1, [ones_f.b, sq2[j].b], [SSP.b])
                rsqrt_mean("dve", r2.t[:], SSP.t[:], 256, [SSP.b], [r2.b])
                for j in range(2):
                    A = ao[aoc % 4]
                    aoc += 1
                    STT("dve", A.t[:], dd[j].t[:], subg.t[:, j:j + 1], r2.t[:], ALU.mult, ALU.mult, [dd[j].b, subg.b, r2.b], [A.b])
                    DMA("sp", aT_s[2 * h + j, :, g * 512:(g + 1) * 512], A.t[:], [A.b], [dbuf("d")], A)
        S.barrier()
    es_ph[0] = es

    with ExitStack() as ph:
        es_ph[0] = ph
        gmem_bc = sb("gmem_bc", [128, D], F32)
        DMA("sp", gmem_bc.t[:], rowv_d[1:2, :].partition_broadcast(128), (), [gmem_bc.b], gmem_bc)
        xt = [sb("xt%d" % i, [128, D], F32) for i in range(2)]
        junk = sb("junk", [128, D], BF16)
        ssq = [sb("ssq%d" % i, [128, 1], F32) for i in range(2)]
        hn = [sb("hn%d" % i, [128, D], BF16) for i in range(2)]
        memT = sb("memT", [128, 16, 512], BF16)
        wkv = [sb("wkv%d" % i, [128, 16, 512], BF16) for i in range(2)]
        mkT = sb("mkT", [128, 4, 512], BF16)
        mv = sb("mv", [128, 4, 512], BF16)
        tp = ps("tp_ps", [128, 16, 128], BF16)
        mmp = [ps("mmp%d" % i, [128, 512], F32) for i in range(2)]
        ssp = ps("ssp", [128, 512], F32)
        op_ = [ps("op%d" % i, [128, 512], F32) for i in range(2)]
        lp_ = [ps("lp", [128, 512], F32)] * 2
        sqx = sb("sqx", [128, 512], F32)
        rx = sb("rx", [128, 512], F32)
        for ti in range(4):
            hf, mt = ti // 2, ti % 2
            X = xt[ti % 2]
            DMA("sp", X.t[:], mem_d[hf, mt * 128:(mt + 1) * 128, :], (), [X.b], X)
            sq_ = ssq[ti % 2]
            ACT(junk.t[:], X.t[:], AF.Square, [X.b], [junk.b, sq_.b], accum_out=sq_.t[:, 0:1])
            rsqrt_mean("dve", sq_.t[:, 0:1], sq_.t[:, 0:1], D, [sq_.b], [sq_.b])
            H = hn[ti % 2]
            STT("dve", H.t[:], X.t[:], sq_.t[:, 0:1], gmem_bc.t[:], ALU.mult, ALU.mult, [X.b, sq_.b, gmem_bc.b], [H.b])
            for k in range(16):
                TR(tp.t[:, k, :], H.t[:, k * 128:(k + 1) * 128], ident_bf.t[:], [H.b, ident_bf.b], [tp.b])
            CP("act", memT.t[:, :, ti * 128:(ti + 1) * 128], tp.t[:], [tp.b], [memT.b])
        for c in range(2):
            DMA("pool", wkv[c].t[:], w_kv_d[:, c * 512:(c + 1) * 512].rearrange("(k p) c -> p k c", p=128), (), [wkv[c].b], wkv[c])
        for hh in range(4):
            P = mmp[hh % 2]
            for k in range(16):
                MM(P.t[:], wkv[0].t[:, k, hh * 128:(hh + 1) * 128], memT.t[:, k, :], k == 0, k == 15, [wkv[0].b, memT.b], [P.b])
            ob = Tl(mkT.t, "x")
            ob.b = mkT.b
            ACT(sqx.t[:], P.t[:], AF.Square, [P.b], [sqx.b])
            MM(ssp.t[:], ones_f.t[:], sqx.t[:], True, True, [ones_f.b, sqx.b], [ssp.b])
            rsqrt_mean("dve", rx.t[:], ssp.t[:], 128, [ssp.b], [rx.b])
            STT("dve", mkT.t[:, hh, :], P.t[:], cvc("xa_k_norm"), rx.t[:], ALU.mult, ALU.mult, [P.b, colv.b, rx.b], [mkT.b])
        for ti in range(4):
            P = mmp[ti % 2]
            for k in range(16):
                MM(P.t[:], memT.t[:, k, ti * 128:(ti + 1) * 128], wkv[1].t[:, k, :], k == 0, k == 15, [wkv[1].b, memT.b], [P.b])
            CP("act", mv.t[:, ti, :], P.t[:], [P.b], [mv.b])
        xq = [sb("xq%d" % i, [128, T], BF16) for i in range(2)]
        Px = [sb("Px%d" % i, [128, 512], BF16) for i in range(3)]
        rlx = sb("rlx", [128, 512], F32)
        co = [sb("co%d" % i, [128, 512], BF16) for i in range(2)]
        it = 0
        for hh in range(4):
            XQ = xq[hh % 2]
            DMA("sp", XQ.t[:], xqnT_s[hh, :, :], [], [XQ.b], XQ)
            for g in range(NG):
                hf = g // NGH
                OP, LP = op_[g % 2], lp_[g % 2]
                for mt in range(2):
                    SP_ = mmp[it % 2]
                    PP = Px[it % 3]
                    it += 1
                    c0 = hf * 256 + mt * 128
                    MM(SP_.t[:], mkT.t[:, hh, c0:c0 + 128], XQ.t[:, g * 512:(g + 1) * 512], True, True, [mkT.b, XQ.b], [SP_.b])
                    ACT(PP.t[:], SP_.t[:], AF.Exp, [SP_.b], [PP.b], scale=scale)
                    MM(OP.t[:], mv.t[:, hf * 2 + mt, hh * 128:(hh + 1) * 128], PP.t[:], mt == 0, mt == 1, [mv.b, PP.b], [OP.b])
                    MM(LP.t[:], ones_bf.t[:], PP.t[:], mt == 0, mt == 1, [ones_bf.b, PP.b], [LP.b])
                S.op("dve", lambda rlx=rlx, LP=LP: nc.vector.reciprocal(out=rlx.t[:], in_=LP.t[:]), [LP.b], [rlx.b])
                C = co[g % 2]
                TT("dve", C.t[:], OP.t[:], rlx.t[:], ALU.mult, [OP.b, rlx.b], [C.b])
                DMA("sp", cT_s[hh, :, g * 512:(g + 1) * 512], C.t[:], [C.b], [dbuf("d")], C)
        S.barrier()
    es_ph[0] = es

    if do_gdn:
        NCH = T // 64
        qk_s = dscr("qk_s", [NCH, 128, 32, 64], BF16)
        kv_s = dscr("kv_s", [NCH, 64, 2, 16, 128], BF16)
        of_s = dscr("of_s", [T, D], F32)
        gconst_d = din("gconst", [64, 6, 64])
        with ExitStack() as ph:
            es_ph[0] = ph
            XW = SH + 4
            xr = [sb("xr%d" % i, [128, 2 * XW], BF16) for i in range(2)]
            for X in xr:
                MSET("pool", X.t[:, 0:2], 0.0, [X.b])
                MSET("pool", X.t[:, 2 * XW - 2:2 * XW], 0.0, [X.b])
            dg = [sb("dg%d" % i, [128, 5, 128], BF16) for i in range(2)]
            sv = [sb("sv%d" % i, [128, 512], F32) for i in range(3)]
            sqv = [sb("sqv%d" % i, [128, 512], F32) for i in range(3)]
            rv = [sb("rv%d" % i, [128, 512], F32) for i in range(3)]
            obv = [sb("obv%d" % i, [128, 512], BF16) for i in range(4)]
            tbv = [sb("tbv%d" % i, [64, 8, 128], BF16) for i in range(2)]
            cps = [ps("cps%d" % i, [128, 512], F32) for i in range(3)]
            ssp = [ps("ssp%d" % i, [128, 512], F32) for i in range(3)]
            trp = [ps("trp%d" % i, [64, 8, 128], F32) for i in range(1)]
            cw = CV["gdn_w"]
            ic = 0
            oc = 0
            gdef = []

            def post_conv(CP_, SV, SQ, RV, SSP, OB, cb, g, oc_):
                if cb < 32:
                    ACT(SV.t[:], CP_.t[:], AF.Silu, [CP_.b], [SV.b])
                    TT("dve", SQ.t[:], SV.t[:], SV.t[:], ALU.mult, [SV.b], [SQ.b])
                    MM(SSP.t[:], ones_f.t[:], SQ.t[:], True, True, [ones_f.b, SQ.b], [SSP.b])
                    rsqrt_mean("dve", RV.t[:], SSP.t[:], 1, [SSP.b], [RV.b])
                    scl = (128.0 ** -0.5) if cb < 16 else 1.0
                    STT("dve", OB.t[:], SV.t[:], scl, RV.t[:], ALU.mult, ALU.mult, [SV.b, RV.b], [OB.b])
                    DMA("sp", qk_s[g * 8:(g + 1) * 8, :, cb, :].rearrange("n p t -> p n t"),
                        OB.t[:].rearrange("p (n t) -> p n t", n=8), [OB.b], [dbuf("d")], OB)
                else:
                    ACT(OB.t[:], CP_.t[:], AF.Silu, [CP_.b], [OB.b])
                if cb >= 16:
                    kk, hh = (0, cb - 16) if cb < 32 else (1, cb - 32)
                    TP = trp[0]
                    for n in range(8):
                        MM(TP.t[:, n, :], OB.t[:, n * 64:(n + 1) * 64], ident_bf.t[:], True, True, [OB.b, ident_bf.b], [TP.b])
                    TB_ = tbv[oc_ % 2]
                    CP("act", TB_.t[:], TP.t[:], [TP.b], [TB_.b])
                    DMA("sp", kv_s[g * 8:(g + 1) * 8, :, kk, hh, :].rearrange("n p d -> p n d"), TB_.t[:], [TB_.b], [dbuf("d")], TB_)

            for cb in range(48):
                X = xr[cb % 2]
                DGT = dg[cb % 2]
                DMA("sp", X.t[:, 2:XW], qkvT_s[cb, :, 0:SH + 2], [], [X.b], X)
                DMA("sp", X.t[:, XW:2 * XW - 2], qkvT_s[cb, :, SH - 2:T], [], [X.b], X)
                TS("dve", X.t[:, XW - 2:XW + 2], X.t[:, XW - 2:XW + 2], flag.t[:, 0:1], None, ALU.mult, None, [X.b, flag.b], [X.b])
                for j in range(5):
                    TS("dve", DGT.t[:, j, :], ident_f.t[:], colv.t[:, cw + j * 48 + cb:cw + j * 48 + cb + 1], None, ALU.mult, None,
                       [ident_f.b, colv.b], [DGT.b])
                for g in range(NG):
                    hf, gl = g // NGH, g % NGH
                    c0 = hf * XW + 2 + gl * 512
                    CP_ = cps[ic % 3]
                    SV, SQ, RV = sv[ic % 3], sqv[ic % 3], rv[ic % 3]
                    SSP = ssp[ic % 3]
                    ic += 1
                    for j in range(5):
                        MM(CP_.t[:], DGT.t[:, j, :], X.t[:, c0 + j - 2:c0 + j - 2 + 512], j == 0, j == 4, [DGT.b, X.b], [CP_.b])
                    while len(gdef) > 1:
                        gdef.pop(0)()
                    OB = obv[oc % 4]
                    oc += 1
                    gdef.append(lambda CP_=CP_, SV=SV, SQ=SQ, RV=RV, SSP=SSP, OB=OB, cb=cb, g=g, oc_=oc: post_conv(CP_, SV, SQ, RV, SSP, OB, cb, g, oc_))
                    continue
                    if cb < 32:
                        ACT(SV.t[:], CP_.t[:], AF.Silu, [CP_.b], [SV.b])
                        TT("dve", SQ.t[:], SV.t[:], SV.t[:], ALU.mult, [SV.b], [SQ.b])
                        MM(SSP.t[:], ones_f.t[:], SQ.t[:], True, True, [ones_f.b, SQ.b], [SSP.b])
                        rsqrt_mean("dve", RV.t[:], SSP.t[:], 1, [SSP.b], [RV.b])
                        scl = (128.0 ** -0.5) if cb < 16 else 1.0
                        STT("dve", OB.t[:], SV.t[:], scl, RV.t[:], ALU.mult, ALU.mult, [SV.b, RV.b], [OB.b])
                        DMA("sp", qk_s[g * 8:(g + 1) * 8, :, cb, :].rearrange("n p t -> p n t"),
                            OB.t[:].rearrange("p (n t) -> p n t", n=8), [OB.b], [dbuf("d")], OB)
                    else:
                        ACT(OB.t[:], CP_.t[:], AF.Silu, [CP_.b], [OB.b])
                    if cb >= 16:
                        kk, hh = (0, cb - 16) if cb < 32 else (1, cb - 32)
                        TP = trp[0]
                        for n in range(8):
                            MM(TP.t[:, n, :], OB.t[:, n * 64:(n + 1) * 64], ident_bf.t[:], True, True, [OB.b, ident_bf.b], [TP.b])
                        TB_ = tbv[oc % 2]
                        CP("act", TB_.t[:], TP.t[:], [TP.b], [TB_.b])
                        DMA("sp", kv_s[g * 8:(g + 1) * 8, :, kk, hh, :].rearrange("n p d -> p n d"), TB_.t[:], [TB_.b], [dbuf("d")], TB_)
            while gdef:
                gdef.pop(0)()
            S.barrier()
        es_ph[0] = es

        with ExitStack() as ph:
            es_ph[0] = ph
            gcn = sb("gcn", [64, 6, 64], F32)
            DMA("sp", gcn.t[:], gconst_d[:, :, :], [], [gcn.b], gcn)
            gnorm_bc = sb("gnorm_bc", [64, 128], F32)
            alog_bc = sb("alog_bc", [64, 32], F32)
            dtb_bc = sb("dtb_bc", [64, 32], F32)
            DMA("sp", gnorm_bc.t[:], rowv_d[3:4, 0:128].partition_broadcast(64), [], [gnorm_bc.b], gnorm_bc)
            DMA("sp", alog_bc.t[:], rowv_d[4:5, 0:32].partition_broadcast(64), [], [alog_bc.b], alog_bc)
            DMA("sp", dtb_bc.t[:], rowv_d[5:6, 0:32].partition_broadcast(64), [], [dtb_bc.b], dtb_bc)
            negA = sb("negA", [64, 32], F32)
            ACT(negA.t[:], alog_bc.t[:], AF.Exp, [alog_bc.b], [negA.b])
            TS("dve", negA.t[:], negA.t[:], -1.0, None, ALU.mult, None, [negA.b], [negA.b])
            negI = sb("negI", [64, 64], F32)
            TS("dve", negI.t[:], ident_f.t[0:64, 0:64], -1.0, None, ALU.mult, None, [ident_f.b], [negI.b])
            SHP = [64, 2, 16, 64]
            Itile = sb("Itile", SHP, BF16)
            CP("dve", Itile.t[:], ident_f.t[0:64, 0:64].unsqueeze(1).unsqueeze(1).to_broadcast(SHP), [ident_f.b], [Itile.b])
            I64b = ident_f.t[0:64, 0:64].unsqueeze(1).unsqueeze(1).to_broadcast(SHP)
            gmask = sb("gmask", [64, 4, 64], BF16)
            CP("dve", gmask.t[:], gcn.t[:, 2:6, :], [gcn.b], [gmask.b])

            QK = [sb("QK%d" % i, [128, 2, 32, 64], BF16) for i in range(2)]
            KV = sb("KV", [64, 2, 2, 16, 128], BF16)
            ABt = sb("ABt", [64, 2, 64], F32)
            bt = [sb("bt%d" % i, [64, 2, 16], F32) for i in range(2)]
            lnb = sb("lnb", [64, 2, 16], F32)
            gg = sb("gg", [64, 2, 16], F32)
            gc = sb("gc", [64, 2, 16], F32)
            gcb = sb("gcb", [64, 2, 16], F32)
            egc = sb("egc", [64, 2, 16], F32)
            nbeg = [sb("nbeg%d" % i, [64, 2, 16], F32) for i in range(2)]
            kdf = [sb("kdf%d" % i, [64, 2, 16], F32) for i in range(2)]
            egt = [sb("egt%d" % i, [128, 2, 16], F32) for i in range(2)]
            G_sb = sb("G_sb", SHP, BF16)
            QKT_sb = sb("QKT_sb", SHP, BF16)
            Gd = sb("Gd", SHP, F32)
            Gc2 = sb("Gc2", SHP, F32)
            eR = sb("eR", SHP, BF16)
            egrow = sb("egrow", [128, 2, 16, 64], BF16)
            qkT = [sb("qkT%d" % i, SHP, BF16) for i in range(2)]
            Nb = [sb("Nb%d" % i, SHP, BF16) for i in range(2)]
            Mb = [sb("Mb%d" % i, SHP, BF16) for i in range(2)]
            P32 = sb("P32", SHP, F32)
            Pbf = [sb("Pbf%d" % i, SHP, BF16) for i in range(2)]
            qdecT = [sb("qdecT%d" % i, [128, 2, 16, 64], BF16) for i in range(2)]
            kdec = sb("kdec", [64, 2, 16, 128], BF16)
            vb = sb("vb", [64, 2, 16, 128], BF16)
            tmpx_ap = Gc2.t[:].rearrange("p c h i -> p (c h i)").rearrange("p (h d) -> p h d", h=16)
            Xc = sb("Xc", [64, 16, 128], BF16)
            vnew = sb("vnew", [64, 16, 128], BF16)
            S32 = sb("S32", [128, 16, 128], F32)
            Sbf = sb("Sbf", [128, 16, 128], BF16)
            ost = sb("ost", [64, 16, 128], F32)
            OFt = sb("OFt", [64, 16, 128], F32)
            Zt = sb("Zt", [64, 16, 128], BF16)
            ssum = sb("ssum", [64, 16], F32)
            dtok = sb("dtok", [64, 16, 128], BF16)
            dTt = sb("dTt", [128, 16, 64], BF16)
            slab = [ps("slabA", [128, 2048], F32), ps("slabB", [128, 2048], F32)]
            PSL, SSL = slab[0], slab[1]

            def v4(tl, np_=64):
                return tl.t[0:np_, :].rearrange("p (c h i) -> p c h i", c=2, h=16)

            def v3(tl, np_=64):
                return tl.t[0:np_, :].rearrange("p (h d) -> p h d", h=16)

            def bc3(tl):
                return tl.t[:, :, :].unsqueeze(3).to_broadcast(SHP)

            def flat(tl):
                return tl.t[:].rearrange("p c h i -> p (c h i)")

            ones64 = ones_f.t[0:64, 0:64]

            def build_R(dst, d1, mask_idx):
                TT("dve", Gd.t[:], bc3(d1), I64b, ALU.mult, [d1.b, ident_f.b], [Gd.b])
                CP("dve", Gc2.t[:], bc3(gc), [gc.b], [Gc2.b])
                for q4 in range(4):
                    cs = slice(q4 * 512, (q4 + 1) * 512)
                    MM(dst.t[0:64, cs], ones64, flat(Gd)[:, cs], True, False, [ones_f.b, Gd.b], [dst.b])
                    MM(dst.t[0:64, cs], negI.t[:], flat(Gc2)[:, cs], False, False, [negI.b, Gc2.b], [dst.b])
                    MM(dst.t[0:64, cs], gmask.t[:, mask_idx - 2, :], flat(Itile)[:, cs], False, True, [gmask.b, Itile.b], [dst.b])

            def mm32(dst, lhs, rhs):
                for c in range(2):
                    for h in range(16):
                        o_ = (c * 16 + h) * 64
                        MM(dst.t[0:64, o_:o_ + 64], lhs.t[:, c, h, :], rhs.t[:, c, h, :], True, True, [lhs.b, rhs.b], [dst.b])

            def prep(n, p, dr):
                QKp = QK[p]
                DMA("sp", QKp.t[:], qk_s[2 * n:2 * n + 2, :, :, :].rearrange("c p b t -> p c b t"), [], [QKp.b], QKp)
                DMA("sp", ABt.t[:], ab_s[n * 128:(n + 1) * 128, :].rearrange("(c p) x -> p c x", p=64), [], [ABt.b], ABt)
                BT, NBG, KDF, EGT = bt[p], nbeg[p], kdf[p], egt[p]
                ACT(BT.t[:], ABt.t[:, :, dr * 16:(dr + 1) * 16], AF.Sigmoid, [ABt.b], [BT.b])
                ACT(lnb.t[:], BT.t[:], AF.Ln, [BT.b], [lnb.b])
                TT("dve", gg.t[:], ABt.t[:, :, 32 + dr * 16:32 + (dr + 1) * 16],
                   dtb_bc.t[:, dr * 16:(dr + 1) * 16].unsqueeze(1).to_broadcast([64, 2, 16]), ALU.add, [ABt.b, dtb_bc.b], [gg.b])
                ACT(gg.t[:], gg.t[:], AF.Exp, [gg.b], [gg.b])
                ACT(gg.t[:], gg.t[:], AF.Ln, [gg.b, ones_f.b], [gg.b], bias=ones_f.t[0:64, 0:1])
                TT("dve", gg.t[:], gg.t[:], negA.t[:, dr * 16:(dr + 1) * 16].unsqueeze(1).to_broadcast([64, 2, 16]), ALU.mult,
                   [gg.b, negA.b], [gg.b])
                yield
                GP = PSL
                ggf = gg.t[:].rearrange("p c h -> p (c h)")
                MM(GP.t[0:64, 0:32], gcn.t[:, dr, :], ggf, True, True, [gcn.b, gg.b], [GP.b])
                MM(GP.t[:, 32:64], ones_f.t[0:64, :], ggf, True, True, [ones_f.b, gg.b], [GP.b])
                CP("dve", gc.t[:].rearrange("p c h -> p (c h)"), GP.t[0:64, 0:32], [GP.b], [gc.b])
                TT("dve", gcb.t[:], gc.t[:], lnb.t[:], ALU.add, [gc.b, lnb.b], [gcb.b])
                ACT(egc.t[:], gc.t[:], AF.Exp, [gc.b], [egc.b])
                STT("dve", NBG.t[:], egc.t[:], -1.0, BT.t[:], ALU.mult, ALU.mult, [egc.b, BT.b], [NBG.b])
                TT("dve", KDF.t[:].rearrange("p c h -> p (c h)"), GP.t[0:64, 32:64], gc.t[:].rearrange("p c h -> p (c h)"),
                   ALU.subtract, [GP.b, gc.b], [KDF.b])
                ACT(KDF.t[:], KDF.t[:], AF.Exp, [KDF.b], [KDF.b])
                ACT(EGT.t[:].rearrange("p c h -> p (c h)"), GP.t[:, 32:64], AF.Exp, [GP.b], [EGT.b])
                yield
                for c in range(2):
                    for h in range(16):
                        o_ = (c * 16 + h) * 64
                        MM(PSL.t[0:64, o_:o_ + 64], QKp.t[:, c, 16 + h, :], QKp.t[:, c, 16 + h, :], True, True, [QKp.b], [PSL.b])
                CP("act", G_sb.t[:], v4(PSL), [PSL.b], [G_sb.b])
                yield
                for c in range(2):
                    for h in range(16):
                        o_ = (c * 16 + h) * 64
                        MM(PSL.t[0:64, o_:o_ + 64], QKp.t[:, c, 16 + h, :], QKp.t[:, c, h, :], True, True, [QKp.b], [PSL.b])
                CP("act", QKT_sb.t[:], v4(PSL), [PSL.b], [QKT_sb.b])
                yield
                TT("dve", Gd.t[:], bc3(gc), I64b, ALU.mult, [gc.b, ident_f.b], [Gd.b])
                for q4 in range(4):
                    cs = slice(q4 * 512, (q4 + 1) * 512)
                    MM(PSL.t[:, cs], ones_f.t[0:64, :], flat(Gd)[:, cs], True, True, [ones_f.b, Gd.b], [PSL.b])
                ACT(egrow.t[:], v4(PSL, 128), AF.Exp, [PSL.b], [egrow.b])
                TT("dve", qdecT[p].t[:], QKp.t[:, :, 0:16, :], egrow.t[:], ALU.mult, [QKp.b, egrow.b], [qdecT[p].b])
                yield
                build_R(PSL, gc, 2 + dr)
                ACT(eR.t[:], v4(PSL), AF.Exp, [PSL.b], [eR.b])
                TT("dve", qkT[p].t[:], QKT_sb.t[:], eR.t[:], ALU.mult, [QKT_sb.b, eR.b], [qkT[p].b])
                yield
                build_R(PSL, gcb, 4 + dr)
                ACT(eR.t[:], v4(PSL), AF.Exp, [PSL.b], [eR.b])
                STT("dve", Nb[0].t[:], G_sb.t[:], -1.0, eR.t[:], ALU.mult, ALU.mult, [G_sb.b, eR.b], [Nb[0].b])
                TT("dve", P32.t[:], Nb[0].t[:], I64b, ALU.add, [Nb[0].b, ident_f.b], [P32.b])
                PB = Pbf[p]
                CP("act", PB.t[:], P32.t[:], [P32.b], [PB.b])
                yield
                for c in range(2):
                    for h in range(16):
                        o_ = (c * 16 + h) * 64
                        MM(PSL.t[0:64, o_:o_ + 64], Nb[0].t[:, c, h, :], ident_bf.t[0:64, 0:64], True, True, [Nb[0].b, ident_bf.b], [PSL.b])
                CP("act", Mb[0].t[:], v4(PSL), [PSL.b], [Mb[0].b])
                yield
                for k in range(1, 6):
                    Np, Mp = Nb[(k - 1) % 2], Mb[(k - 1) % 2]
                    Nn, Mn = Nb[k % 2], Mb[k % 2]
                    mm32(PSL, Np, Mp)
                    CP("act", Mn.t[:], v4(PSL), [PSL.b], [Mn.b])
                    yield
                    if k < 5:
                        mm32(PSL, Mp, Np)
                        CP("dve", Nn.t[:], v4(PSL), [PSL.b], [Nn.b])
                        yield
                    mm32(PSL, Mn, PB)
                    TT("dve", P32.t[:], P32.t[:], v4(PSL), ALU.add, [P32.b, PSL.b], [P32.b])
                    CP("act", PB.t[:], P32.t[:], [P32.b], [PB.b])
                    yield

            def steps(n, p, dr):
                QKp, PB, BT, NBG, KDF, EGT = QK[p], Pbf[p], bt[p], nbeg[p], kdf[p], egt[p]
                DMA("sp", KV.t[:], kv_s[2 * n:2 * n + 2, :, :, :, :].rearrange("c p k h d -> p c k h d"), [], [KV.b], KV)
                TT("dve", vb.t[:], KV.t[:, :, 1, :, :], BT.t[:, :, :].unsqueeze(3).to_broadcast([64, 2, 16, 128]), ALU.mult,
                   [KV.b, BT.b], [vb.b])
                TT("dve", kdec.t[:], KV.t[:, :, 0, :, :], KDF.t[:, :, :].unsqueeze(3).to_broadcast([64, 2, 16, 128]), ALU.mult,
                   [KV.b, KDF.b], [kdec.b])
                yield
                for c in ([0, 1] if dr == 0 else [1, 0]):
                    ch = 2 * n + c
                    r0 = ch * 64
                    for h in range(16):
                        MM(SSL.t[0:64, h * 128:(h + 1) * 128], QKp.t[:, c, 16 + h, :], Sbf.t[:, h, :], True, True, [QKp.b, Sbf.b], [SSL.b])
                    TT("dve", tmpx_ap, v3(SSL), NBG.t[:, c, :].unsqueeze(2).to_broadcast([64, 16, 128]), ALU.mult, [SSL.b, NBG.b], [Gc2.b])
                    TT("dve", Xc.t[:], tmpx_ap, vb.t[:, c, :, :], ALU.add, [Gc2.b, vb.b], [Xc.b])
                    yield
                    for h in range(16):
                        MM(SSL.t[0:64, h * 128:(h + 1) * 128], PB.t[:, c, h, :], Xc.t[:, h, :], True, True, [PB.b, Xc.b], [SSL.b])
                    CP("act", vnew.t[:], v3(SSL), [SSL.b], [vnew.b])
                    yield
                    for h in range(16):
                        MM(SSL.t[0:64, h * 128:(h + 1) * 128], qdecT[p].t[:, c, h, :], Sbf.t[:, h, :], True, False, [qdecT[p].b, Sbf.b], [SSL.b])
                        MM(SSL.t[0:64, h * 128:(h + 1) * 128], qkT[p].t[:, c, h, :], vnew.t[:, h, :], False, True, [qkT[p].b, vnew.b], [SSL.b])
                    if dr == 0:
                        CP("act", ost.t[:], v3(SSL), [SSL.b], [ost.b])
                        DMA("sp", of_s[r0:r0 + 64, :], ost.t[:].rearrange("p h d -> p (h d)"), [ost.b], [dbuf("d")], ost)
                        yield
                    else:
                        DMA("sp", OFt.t[:].rearrange("p h d -> p (h d)"), of_s[r0:r0 + 64, :], [], [OFt.b], OFt)
                        DMA("sp", Zt.t[:].rearrange("p h d -> p (h d)"), z_s[r0:r0 + 64, :], [], [Zt.b], Zt)
                        TT("dve", ost.t[:], v3(SSL), OFt.t[:], ALU.add, [SSL.b, OFt.b], [ost.b])
                        yield
                    for h in range(16):
                        MM(SSL.t[:, h * 128:(h + 1) * 128], kdec.t[:, c, h, :], vnew.t[:, h, :], True, True, [kdec.b, vnew.b], [SSL.b])
                    TT("dve", S32.t[:], S32.t[:], EGT.t[:, c, :].unsqueeze(2).to_broadcast([128, 16, 128]), ALU.mult, [S32.b, EGT.b], [S32.b])
                    TT("dve", S32.t[:], S32.t[:], v3(SSL, 128), ALU.add, [S32.b, SSL.b], [S32.b])
                    nxt_ch = ch + 1 if dr == 0 else ch - 1
                    if (dr == 0 and nxt_ch == NCH // 2) or (dr == 1 and ch == NCH // 2):
                        TS("dve", S32.t[:], S32.t[:], flag.t[:, 0:1], None, ALU.mult, None, [S32.b, flag.b], [S32.b])
                    CP("act", Sbf.t[:], S32.t[:], [S32.b], [Sbf.b])
                    yield
                    if dr == 1:
                        TT("dve", tmpx_ap, ost.t[:], ost.t[:], ALU.mult, [ost.b], [Gc2.b])
                        S.op("dve", lambda: nc.vector.reduce_sum(out=ssum.t[:], in_=tmpx_ap, axis=AX.X), [Gc2.b], [ssum.b])
                        rsqrt_mean("dve", ssum.t[:], ssum.t[:], 128, [ssum.b], [ssum.b])
                        TT("dve", ost.t[:], ost.t[:], ssum.t[:, :].unsqueeze(2).to_broadcast([64, 16, 128]), ALU.mult, [ost.b, ssum.b], [ost.b])
                        TT("dve", ost.t[:], ost.t[:], gnorm_bc.t[:, :].unsqueeze(1).to_broadcast([64, 16, 128]), ALU.mult,
                           [ost.b, gnorm_bc.b], [ost.b])
                        TT("dve", dtok.t[:], ost.t[:], Zt.t[:], ALU.mult, [ost.b, Zt.b], [dtok.b])
                        yield
                        for h in range(16):
                            MM(SSL.t[:, h * 64:(h + 1) * 64], dtok.t[:, h, :], ident_bf.t[0:64, 0:64], True, True, [dtok.b, ident_bf.b], [SSL.b])
                        CP("act", dTt.t[:], SSL.t[:, 0:1024].rearrange("p (h t) -> p h t", h=16), [SSL.b], [dTt.b])
                        DMA("sp", dT_s[:, :, r0:r0 + 64].rearrange("h p t -> p h t"), dTt.t[:], [dTt.b], [dbuf("d")], dTt)
                        yield

            def run_gens(*gens):
                alive = [g_ for g_ in gens if g_ is not None]
                while alive:
                    for g_ in list(alive):
                        try:
                            next(g_)
                        except StopIteration:
                            alive.remove(g_)

            for dr in range(2):
                MSET("dve", S32.t[:], 0.0, [S32.b])
                MSET("dve", Sbf.t[:], 0.0, [Sbf.b])
                tiles = list(range(NT)) if dr == 0 else list(range(NT - 1, -1, -1))
                run_gens(prep(tiles[0], 0, dr))
                for ti in range(NT):
                    nxt_prep = prep(tiles[ti + 1], (ti + 1) % 2, dr) if ti + 1 < NT else None
                    run_gens(steps(tiles[ti], ti % 2, dr), nxt_prep)
            S.barrier()
        es_ph[0] = es

    with ExitStack() as ph:
        es_ph[0] = ph
        gffn_bc = sb("gffn_bc", [128, D], F32)
        DMA("sp", gffn_bc.t[:], rowv_d[2:3, :].partition_broadcast(128), (), [gffn_bc.b], gffn_bc)
        aT = sb("aT", [128, 16, 512], BF16)
        dT = sb("dT", [128, 16, 512], BF16)
        cT = sb("cT", [128, 4, 512], BF16)
        gch = [sb("gch%d" % i, [128, 3, 2, 512], BF16) for i in range(2)]
        wpa = [sb("wpa%d" % i, [128, 16, 256], BF16) for i in range(2)]
        wpd = [sb("wpd%d" % i, [128, 16, 256], BF16) for i in range(2)]
        wpc = [sb("wpc%d" % i, [128, 4, 256], BF16) for i in range(2)]
        mT = sb("mT", [128, 16, 512], BF16)
        wo = [sb("wo%d" % i, [128, 16, 256], BF16) for i in range(2)]
        xin = [sb("xin%d" % i, [128, D], F32) for i in range(4)]
        junkf = sb("junkf", [128, D], F32)
        ssq = [sb("ssq%d" % i, [128, 1], F32) for i in range(2)]
        h2 = [sb("h2_%d" % i, [128, D], BF16) for i in range(2)]
        h2T = [sb("h2T%d" % i, [128, 16, 128], BF16) for i in range(2)]
        t1 = [sb("t1_%d" % i, [128, 512], F32) for i in range(2)]
        t2 = [sb("t2_%d" % i, [128, 512], F32) for i in range(2)]
        t3 = [sb("t3_%d" % i, [128, 512], F32) for i in range(2)]
        pa = [ps("pa%d" % i, [128, 512], F32) for i in range(2)]
        pd = ps("pd", [128, 512], F32)
        pc = ps("pc", [128, 512], F32)
        po = [ps("po%d" % i, [128, 256], F32) for i in range(2)]
        tp = ps("tp_ps", [128, 16, 128], BF16)
        wc = 0
        woc = 0
        tcn = 0
        for g in range(NG):
            t0 = g * 512
            DMA("sp", aT.t[:], aT_s[:, :, t0:t0 + 512].rearrange("k p t -> p k t"), [], [aT.b], aT)
            if do_gdn:
                DMA("sp", dT.t[:], dT_s[:, :, t0:t0 + 512].rearrange("k p t -> p k t"), [], [dT.b], dT)
            DMA("sp", cT.t[:], cT_s[:, :, t0:t0 + 512].rearrange("k p t -> p k t"), [], [cT.b], cT)
            for cc in range(8):
                Wa, Wd_, Wc = wpa[wc % 2], wpd[wc % 2], wpc[wc % 2]
                G = gch[wc % 2]
                wc += 1
                cs = slice(cc * 256, (cc + 1) * 256)
                DMA("pool", Wa.t[:], p_attn_d[:, cs].rearrange("(k p) c -> p k c", p=128), wbufs, [Wa.b], Wa)
                if do_gdn:
                    DMA("pool", Wd_.t[:], p_gdn_d[:, cs].rearrange("(k p) c -> p k c", p=128), wbufs, [Wd_.b], Wd_)
                DMA("pool", Wc.t[:], p_mem_d[:, cs].rearrange("(k p) c -> p k c", p=128), wbufs, [Wc.b], Wc)
                for br in range(3):
                    DMA("sp", G.t[:, br, :, :], gT_s[br * 16 + cc * 2: br * 16 + cc * 2 + 2, :, t0:t0 + 512].rearrange("k p t -> p k t"),
                        [], [G.b], G)
                for f in range(2):
                    fc = cc * 2 + f
                    PA = pa[fc % 2]
                    for k in range(16):
                        MM(PA.t[:], Wa.t[:, k, f * 128:(f + 1) * 128], aT.t[:, k, :], k == 0, k == 15, [Wa.b, aT.b], [PA.b])
                    if do_gdn:
                        for k in range(16):
                            MM(pd.t[:], Wd_.t[:, k, f * 128:(f + 1) * 128], dT.t[:, k, :], k == 0, k == 15, [Wd_.b, dT.b], [pd.b])
                    for k in range(4):
                        MM(pc.t[:], Wc.t[:, k, f * 128:(f + 1) * 128], cT.t[:, k, :], k == 0, k == 3, [Wc.b, cT.b], [pc.b])
                    A1, A2, A3 = t1[tcn % 2], t2[tcn % 2], t3[tcn % 2]
                    tcn += 1
                    TT("dve", A1.t[:], PA.t[:], G.t[:, 0, f, :], ALU.mult, [PA.b, G.b], [A1.b])
                    TT("dve", A3.t[:], pc.t[:], G.t[:, 2, f, :], ALU.mult, [pc.b, G.b], [A3.b])
                    if do_gdn:
                        TT("dve", A2.t[:], pd.t[:], G.t[:, 1, f, :], ALU.mult, [pd.b, G.b], [A2.b])
                        TT("pool", A1.t[:], A1.t[:], A2.t[:], ALU.add, [A1.b, A2.b], [A1.b])
                    TT("pool", mT.t[:, fc, :], A1.t[:], A3.t[:], ALU.add, [A1.b, A3.b], [mT.b])
            for tt in range(4):
                r0 = t0 + tt * 128
                DMA("sp", xin[tt].t[:], x_d[r0:r0 + 128, :], [], [xin[tt].b], xin[tt])
            for cc in range(8):
                WO = wo[woc % 2]
                woc += 1
                DMA("pool", WO.t[:], w_o_d[:, cc * 256:(cc + 1) * 256].rearrange("(k p) c -> p k c", p=128), wbufs, [WO.b], WO)
                for tt in range(4):
                    XI = xin[tt]
                    PO = po[(cc * 4 + tt) % 2]
                    for k in range(16):
                        MM(PO.t[:], mT.t[:, k, tt * 128:(tt + 1) * 128], WO.t[:, k, :], k == 0, k == 15, [mT.b, WO.b], [PO.b])
                    TT("dve", XI.t[:, cc * 256:(cc + 1) * 256], PO.t[:], XI.t[:, cc * 256:(cc + 1) * 256], ALU.add, [PO.b, XI.b], [XI.b])
            for tt in range(4):
                r0 = t0 + tt * 128
                X1 = xin[tt]
                DMA("sp", x1_s[r0:r0 + 128, :], X1.t[:], [X1.b], [dbuf("d")], X1)
                sq_ = ssq[tt % 2]
                TT("dve", junkf.t[:], X1.t[:], X1.t[:], ALU.mult, [X1.b], [junkf.b])
                S.op("dve", lambda sq_=sq_: nc.vector.reduce_sum(out=sq_.t[:, 0:1], in_=junkf.t[:], axis=AX.X), [junkf.b], [sq_.b])
                rsqrt_mean("dve", sq_.t[:, 0:1], sq_.t[:, 0:1], D, [sq_.b], [sq_.b])
                H = h2[tt % 2]
                STT("dve", H.t[:], X1.t[:], sq_.t[:, 0:1], gffn_bc.t[:], ALU.mult, ALU.mult, [X1.b, sq_.b, gffn_bc.b], [H.b])
                for k in range(16):
                    TR(tp.t[:, k, :], H.t[:, k * 128:(k + 1) * 128], ident_bf.t[:], [H.b, ident_bf.b], [tp.b])
                HT = h2T[tt % 2]
                CP("act", HT.t[:], tp.t[:], [tp.b], [HT.b])
                DMA("sp", h2T_s[:, :, r0:r0 + 128].rearrange("k p t -> p k t"), HT.t[:], [HT.b], [dbuf("d")], HT)
        S.barrier()
    es_ph[0] = es

    with ExitStack() as ph:
        es_ph[0] = ph
        hT2 = [sb("hT2_%d" % i, [128, 16, 514], BF16) for i in range(1)]
        wg = [sb("wg%d" % i, [128, 16, 256], BF16) for i in range(2)]
        wu = [sb("wu%d" % i, [128, 16, 256], BF16) for i in range(2)]
        actT = sb("actT", [128, 44, 512], BF16)
        gts = [sb("gts%d" % i, [128, 514], F32) for i in range(2)]
        a1 = [sb("a1_%d" % i, [128, 512], F32) for i in range(2)]
        a2 = [sb("a2_%d" % i, [128, 512], F32) for i in range(2)]
        wd = [sb("wd%d" % i, [128, 44, 256], BF16) for i in range(2)]
        x1t = [sb("x1t%d" % i, [128, D], F32) for i in range(4)]
        pg = [ps("pg%d" % i, [128, 512], F32) for i in range(2)]
        pu = [ps("pu%d" % i, [128, 512], F32) for i in range(2)]
        ph_ = [ps("ph%d" % i, [128, 2], F32) for i in range(2)]
        py = [ps("py%d" % i, [128, 256], F32) for i in range(2)]
        wcn = 0
        wdc = 0
        bc = 0
        cw, cbb = CV["ffn_w"], CV["ffn_b"]
        for g in range(NG):
            t0 = g * 512
            HT = hT2[0]
            first = (g % NGH == 0)
            last = (g % NGH == NGH - 1)
            lo = t0 - 1 if t0 > 0 else t0
            hi = t0 + 513 if t0 + 513 <= T else t0 + 512
            DMA("sp", HT.t[:, :, (1 - (t0 - lo)):(1 + hi - t0)], h2T_s[:, :, lo:hi].rearrange("k p t -> p k t"), [], [HT.b], HT)
            if t0 == 0:
                MSET("pool", HT.t[:, :, 0:1], 0.0, [HT.b])
            elif first:
                TS("pool", HT.t[:, :, 0:1], HT.t[:, :, 0:1], flag.t[:, 0:1], None, ALU.mult, None, [HT.b, flag.b], [HT.b])
            if t0 + 512 == T:
                MSET("pool", HT.t[:, :, 513:514], 0.0, [HT.b])
            elif last:
                TS("pool", HT.t[:, :, 513:514], HT.t[:, :, 513:514], flag.t[:, 0:1], None, ALU.mult, None, [HT.b, flag.b], [HT.b])
            for cc in range(22):
                WG, WU = wg[wcn % 2], wu[wcn % 2]
                wcn += 1
                DMA("pool", WG.t[:], w_up_d[:, cc * 256:(cc + 1) * 256].rearrange("(k p) c -> p k c", p=128), wbufs, [WG.b], WG)
                DMA("pool", WU.t[:], w_up_d[:, DFF + cc * 256:DFF + (cc + 1) * 256].rearrange("(k p) c -> p k c", p=128), wbufs, [WU.b], WU)
                for f in range(2):
                    cb_ = cc * 2 + f
                    PG, PU, PH = pg[bc % 2], pu[bc % 2], ph_[bc % 2]
                    GT, A1, A2 = gts[bc % 2], a1[bc % 2], a2[bc % 2]
                    bc += 1
                    for k in range(16):
                        MM(PG.t[:], WG.t[:, k, f * 128:(f + 1) * 128], HT.t[:, k, 1:513], k == 0, k == 15, [WG.b, HT.b], [PG.b])
                    for k in range(16):
                        MM(PH.t[:], WG.t[:, k, f * 128:(f + 1) * 128], HT.t[:, k, 0:514:513], k == 0, k == 15, [WG.b, HT.b], [PH.b])
                    for k in range(16):
                        MM(PU.t[:], WU.t[:, k, f * 128:(f + 1) * 128], HT.t[:, k, 1:513], k == 0, k == 15, [WU.b, HT.b], [PU.b])
                    CP("act", GT.t[:, 1:513], PG.t[:], [PG.b], [GT.b])
                    CP("act", GT.t[:, 0:514:513], PH.t[:], [PH.b], [GT.b])
                    ACT(A1.t[:], GT.t[:, 0:512], AF.Identity, [GT.b, colv.b], [A1.b],
                        bias=colv.t[:, cbb + cb_:cbb + cb_ + 1], scale=colv.t[:, cw + cb_:cw + cb_ + 1])
                    STT("dve", A2.t[:], GT.t[:, 1:513], colv.t[:, cw + 44 + cb_:cw + 44 + cb_ + 1], A1.t[:], ALU.mult, ALU.add,
                        [GT.b, colv.b, A1.b], [A2.b])
                    STT("dve", A1.t[:], GT.t[:, 2:514], colv.t[:, cw + 88 + cb_:cw + 88 + cb_ + 1], A2.t[:], ALU.mult, ALU.add,
                        [GT.b, colv.b, A2.b], [A1.b])
                    ACT(A2.t[:], A1.t[:], AF.Silu, [A1.b], [A2.b])
                    TT("dve", actT.t[:, cb_, :], PU.t[:], A2.t[:], ALU.mult, [PU.b, A2.b], [actT.b])
            for tt in range(4):
                r0 = t0 + tt * 128
                DMA("sp", x1t[tt].t[:], x1_s[r0:r0 + 128, :], [], [x1t[tt].b], x1t[tt])
            for cc in range(8):
                WD = wd[wdc % 2]
                wdc += 1
                DMA("pool", WD.t[:], w_dn_d[:, cc * 256:(cc + 1) * 256].rearrange("(k p) c -> p k c", p=128), wbufs, [WD.b], WD)
                for tt in range(4):
                    X1 = x1t[tt]
                    PY = py[(cc * 4 + tt) % 2]
                    for k in range(44):
                        MM(PY.t[:], actT.t[:, k, tt * 128:(tt + 1) * 128], WD.t[:, k, :], k == 0, k == 43, [actT.b, WD.b], [PY.b])
                    TT("dve", X1.t[:, cc * 256:(cc + 1) * 256], PY.t[:], X1.t[:, cc * 256:(cc + 1) * 256], ALU.add, [PY.b, X1.b], [X1.b])
            for tt in range(4):
                r0 = t0 + tt * 128
                DMA("sp", y_d[r0:r0 + 128, :], x1t[tt].t[:], [x1t[tt].b], [dbuf("d")], x1t[tt])
    es_ph[0] = es
    S.emit()
    es.close()
    return nc, S


def build_gdn(env):
    raise NotImplementedError


def make_cb(SH, joined):
    T = 2 * SH
    NG, NKC = T // 512, T // 128
    cb = np.zeros((8, NG * NKC), np.float32)
    for h in range(8):
        sl = SLOPES[h]
        for g in range(NG):
            for kc in range(NKC):
                dlt = g * 512 - kc * 128
                cross = (g * 512) // SH != (kc * 128) // SH
                if cross and not joined:
                    v = NEG
                elif dlt >= 127 or dlt <= -511:
                    v = -sl * abs(dlt)
                else:
                    v = 0.0
                cb[h, g * NKC + kc] = v
    return np.ascontiguousarray(np.broadcast_to(cb[:, None, :], (8, 128, NG * NKC)))


def make_core_inputs(SH, xs, mems, joined, W):
    colv = np.zeros((128, 480), np.float32)
    c = 0

    def put(v, n):
        nonlocal c
        colv[:, c:c + n] = v
        c += n
    put(W["da_q_norm"].reshape(128, 1), 1)
    put(W["da_k_norm"].reshape(128, 1), 1)
    put(W["xa_q_norm"].reshape(128, 1), 1)
    put(W["xa_k_norm"].reshape(128, 1), 1)
    put(W["da_subln"].reshape(2, 128).T, 2)
    put(W["da_lambda"].reshape(4, 128).T, 4)
    put(W["b_gate"].reshape(48, 128).T, 48)
    put(W["ffn_conv_b"].reshape(44, 128).T, 44)
    put(W["ffn_conv_w"].reshape(3 * 44, 128).T, 132)
    put(W["gdn_conv_w"].reshape(5 * 48, 128).T, 240)
    rowv = np.zeros((8, D), np.float32)
    rowv[0] = W["g_mix"]
    rowv[1] = W["g_mem"]
    rowv[2] = W["g_ffn"]
    rowv[3, :128] = W["gdn_out_norm"]
    rowv[4, :32] = W["gdn_A_log"].reshape(-1)
    rowv[5, :32] = W["gdn_dt_bias"].reshape(-1)
    flag = np.zeros((128, 2), np.float32)
    flag[:, 0] = 1.0 if joined else 0.0
    flag[:, 1] = 0.0 if joined else 1.0
    d = {
        "x": np.ascontiguousarray(xs, dtype=np.float32),
        "mem": np.ascontiguousarray(mems, dtype=np.float32),
        "w_in": W["w_in"], "w_mem_kv": W["w_mem_kv"], "p_attn": W["p_attn"], "p_gdn": W["p_gdn"],
        "p_mem": W["p_mem"], "w_o": W["w_o"], "w_up": W["w_up"], "w_down": W["w_down"],
        "colv": colv, "rowv": rowv, "cb": make_cb(SH, joined), "flag": flag,
    }
    if DO_GDN:
        d["gconst"] = make_gconst()
    return d


def make_gconst():
    i = np.arange(64)[:, None]
    j = np.arange(64)[None, :]
    gcst = np.zeros((64, 6, 64), np.float32)
    gcst[:, 0, :] = (i <= j)
    gcst[:, 1, :] = (i >= j)
    gcst[:, 2, :] = np.where(i >= j, 0.0, NEG)
    gcst[:, 3, :] = np.where(i <= j, 0.0, NEG)
    gcst[:, 4, :] = np.where(i > j, 0.0, NEG)
    gcst[:, 5, :] = np.where(i < j, 0.0, NEG)
    return gcst


WNAMES = ["g_mix", "g_mem", "w_in", "b_gate", "da_q_norm", "da_k_norm", "da_lambda", "da_subln", "gdn_conv_w",
          "gdn_A_log", "gdn_dt_bias", "gdn_out_norm", "xa_q_norm", "xa_k_norm", "w_mem_kv", "p_attn", "p_gdn",
          "p_mem", "w_o", "g_ffn", "w_up", "ffn_conv_w", "ffn_conv_b", "w_down"]

_NC_CACHE = {}
DO_GDN = True


def kernel(x_prompt, x_sample, mem_prompt, mem_sample, **weights):
    SH = 4096
    W = {k: np.ascontiguousarray(np.asarray(weights[k], dtype=np.float32)[0]) for k in WNAMES}
    xp = np.asarray(x_prompt, dtype=np.float32)
    xs = np.asarray(x_sample, dtype=np.float32)
    mp = np.asarray(mem_prompt, dtype=np.float32)
    ms = np.asarray(mem_sample, dtype=np.float32)
    cores = []
    cores.append(make_core_inputs(SH, xp[0:2].reshape(2 * SH, D), mp[0:2], False, W))
    cores.append(make_core_inputs(SH, xp[2:4].reshape(2 * SH, D), mp[2:4], False, W))
    cores.append(make_core_inputs(SH, xs[0], np.stack([ms[0], ms[0]]), True, W))
    cores.append(make_core_inputs(SH, xs[1], np.stack([ms[1], ms[1]]), True, W))
    in_maps = cores + cores
    if SH not in _NC_CACHE:
        _NC_CACHE[SH] = build(SH, do_gdn=DO_GDN)[0]
    nc = _NC_CACHE[SH]
    res = run_bass_kernel_spmd(nc, in_maps, core_ids=list(range(8)))
    ys = [np.asarray(res.results[i]["y"], dtype=np.float32) for i in range(4)]
    y_prompt = np.concatenate([ys[0].reshape(2, SH, D), ys[1].reshape(2, SH, D)], axis=0)
    y_sample = np.stack([ys[2], ys[3]], axis=0)
    return (y_prompt, y_sample)
```

```python
import math
import numpy as np
import concourse.bass as bass
import concourse.mybir as mybir
from concourse.bass_utils import run_bass_kernel_spmd
from contextlib import ExitStack

F32 = mybir.dt.float32
BF16 = mybir.dt.bfloat16
AF = mybir.ActivationFunctionType
ALU = mybir.AluOpType
AX = mybir.AxisListType

D = 2048
W_IN = 21056
DFF = 5632
EPS = 1e-6
NEG = -30000.0
SLOPES = [2.0 ** (-(h + 1)) for h in range(8)]
LAMBDA_INIT = 0.8 - 0.6 * math.exp(0.0)


class Buf:
    __slots__ = ("name", "w", "rs")

    def __init__(self, name):
        self.name = name
        self.w = None
        self.rs = []


class Op:
    __slots__ = ("eng", "fn", "dma", "deps", "need_inc", "sem", "val", "stream")


class Sched:
    def __init__(self, nc, es):
        self.nc = nc
        self.es = es
        self.ops = []
        self.eng = {"pe": nc.tensor, "act": nc.scalar, "dve": nc.vector, "pool": nc.gpsimd, "sp": nc.sync}
        self.last = {}
        self.pending_barrier = {}

    def op(self, eng, fn, reads=(), writes=(), dma=False, stream=None):
        o = Op()
        o.eng = eng
        o.fn = fn
        o.dma = dma
        o.need_inc = dma
        o.sem = None
        o.val = 0
        o.stream = stream
        deps = set()
        for b in reads:
            if b.w is not None:
                deps.add(b.w)
        for b in writes:
            if b.w is not None:
                deps.add(b.w)
            deps.update(b.rs)
        pb = self.pending_barrier.pop(eng, None)
        if pb:
            deps.update(pb)
        for b in reads:
            b.rs.append(o)
        for b in writes:
            b.w = o
            b.rs = []
        deps.discard(o)
        o.deps = deps
        self.ops.append(o)
        if dma:
            self.last[("dma", id(stream))] = o
        else:
            self.last[eng] = o
        return o

    def barrier(self):
        tails = set(self.last.values())
        for e in self.eng:
            s = self.pending_barrier.setdefault(e, set())
            s.update(tails)

    @staticmethod
    def _needs_wait(o, d):
        if (not o.dma) and (not d.dma) and o.eng == d.eng and o.eng == "pe":
            return False
        return True

    def emit(self):
        nc, es = self.nc, self.es
        for o in self.ops:
            for d in o.deps:
                if self._needs_wait(o, d):
                    d.need_inc = True
        engsem = {e: es.enter_context(nc.semaphore("sem_" + e)) for e in self.eng}
        cnt = {e: 0 for e in self.eng}
        streams = {}
        for o in self.ops:
            if o.dma:
                k = id(o.stream)
                if k not in streams:
                    streams[k] = [es.enter_context(nc.semaphore("dsem%d" % len(streams))), 0]
                st = streams[k]
                st[1] += 16
                o.sem, o.val = st[0], st[1]
            elif o.need_inc:
                cnt[o.eng] += 1
                o.sem, o.val = engsem[o.eng], cnt[o.eng]
        self.n_sems = len(streams) + len(engsem)
        waited = {e: {} for e in self.eng}
        nwait = 0
        for o in self.ops:
            e = self.eng[o.eng]
            w = {}
            for d in o.deps:
                if self._needs_wait(o, d):
                    k = id(d.sem)
                    if k not in w or w[k][1] < d.val:
                        w[k] = (d.sem, d.val)
            wd = waited[o.eng]
            for k, (sem, val) in w.items():
                if wd.get(k, 0) < val:
                    e.wait_ge(sem, val)
                    wd[k] = val
                    nwait += 1
            ins = o.fn()
            if o.need_inc:
                ins.then_inc(o.sem, 16 if o.dma else 1)
        sp = self.eng["sp"]
        for k, (sem, val) in streams.items():
            if waited["sp"].get(id(sem), 0) < val:
                sp.wait_ge(sem, val)
        self.nwait = nwait


class Tl:
    __slots__ = ("t", "b")

    def __init__(self, t, name):
        self.t = t
        self.b = Buf(name)


def build(SH, debug=False, do_gdn=True):
    T = 2 * SH
    NT = T // 128
    NG = T // 512
    NGH = SH // 512
    NKC = T // 128
    TB = min(2048, T)
    NTB = T // TB
    nc = bass.Bass("TRN2", target_bir_lowering=False)
    es = ExitStack()
    S = Sched(nc, es)

    def din(name, shape, dt=F32):
        return nc.dram_tensor(name, list(shape), dt, kind="ExternalInput").ap()

    def dscr(name, shape, dt):
        return nc.dram_tensor(name, list(shape), dt, kind="ExternalOutput" if debug else "Internal").ap()

    x_d = din("x", [T, D])
    mem_d = din("mem", [2, 256, D])
    w_in_d = din("w_in", [D, W_IN])
    w_kv_d = din("w_mem_kv", [D, 1024])
    p_attn_d = din("p_attn", [D, D])
    p_gdn_d = din("p_gdn", [D, D])
    p_mem_d = din("p_mem", [512, D])
    w_o_d = din("w_o", [D, D])
    w_up_d = din("w_up", [D, 2 * DFF])
    w_dn_d = din("w_down", [DFF, D])
    NCV = 480
    colv_d = din("colv", [128, NCV])
    rowv_d = din("rowv", [8, D])
    cb_d = din("cb", [8, 128, NG * NKC])
    flag_d = din("flag", [128, 2])
    y_d = nc.dram_tensor("y", [T, D], F32, kind="ExternalOutput").ap()

    qnT_s = dscr("qnT_s", [16, 128, T], BF16)
    knT_s = dscr("knT_s", [16, 128, T], BF16)
    v_s = dscr("v_s", [T, D], BF16)
    qkvT_s = dscr("qkvT_s", [48, 128, T], BF16)
    z_s = dscr("z_s", [T, D], BF16)
    ab_s = dscr("ab_s", [T, 64], F32)
    xqnT_s = dscr("xqnT_s", [4, 128, T], BF16)
    gT_s = dscr("gT_s", [48, 128, T], BF16)
    aT_s = dscr("aT_s", [16, 128, T], BF16)
    dT_s = dscr("dT_s", [16, 128, T], BF16)
    cT_s = dscr("cT_s", [4, 128, T], BF16)
    h2T_s = dscr("h2T_s", [16, 128, T], BF16)
    x1_s = dscr("x1_s", [T, D], F32)

    CV = {}
    _c = [0]

    def cv_alloc(name, n):
        CV[name] = _c[0]
        _c[0] += n
    cv_alloc("da_q_norm", 1); cv_alloc("da_k_norm", 1); cv_alloc("xa_q_norm", 1); cv_alloc("xa_k_norm", 1)
    cv_alloc("da_subln", 2); cv_alloc("da_lambda", 4); cv_alloc("b_gate", 48); cv_alloc("ffn_b", 44)
    cv_alloc("ffn_w", 132); cv_alloc("gdn_w", 240)
    assert _c[0] <= NCV

    es_ph = [None]
    uid = [0]

    def sb(name, shape, dt):
        uid[0] += 1
        name = "sb%d_%s" % (uid[0], name)
        return Tl(es_ph[0].enter_context(nc.sbuf_tensor(name, list(shape), dt)), name)

    def ps(name, shape, dt=F32):
        uid[0] += 1
        name = "ps%d_%s" % (uid[0], name)
        return Tl(es_ph[0].enter_context(nc.psum_tensor(name, list(shape), dt)), name)

    def dbuf(name):
        uid[0] += 1
        return Buf(name + str(uid[0]))

    def DMA(eng, out_ap, in_ap, reads, writes, stream):
        e = S.eng[eng]
        return S.op(eng, lambda: e.dma_start(out=out_ap, in_=in_ap), reads, writes, dma=True, stream=stream)

    def MM(out_ap, lhsT, rhs, start, stop, reads, writes):
        return S.op("pe", lambda: nc.tensor.matmul(out_ap, lhsT, rhs, start=start, stop=stop), reads, writes)

    def TR(out_ap, in_ap, ident_ap, reads, writes):
        return S.op("pe", lambda: nc.tensor.transpose(out_ap, in_ap, ident_ap), reads, writes)

    def ACT(out_ap, in_ap, func, reads, writes, bias=None, scale=None, accum_out=None):
        kw = {}
        if bias is not None:
            kw["bias"] = bias
        if scale is not None:
            kw["scale"] = scale
        if accum_out is not None:
            kw["accum_out"] = accum_out
        return S.op("act", lambda: nc.scalar.activation(out=out_ap, in_=in_ap, func=func, **kw), reads, writes)

    def veng(eng):
        return nc.vector if eng == "dve" else nc.gpsimd

    def TT(eng, out_ap, in0, in1, op, reads, writes):
        e = veng(eng)
        return S.op(eng, lambda: e.tensor_tensor(out=out_ap, in0=in0, in1=in1, op=op), reads, writes)

    def TS(eng, out_ap, in0, s1, s2, op0, op1, reads, writes):
        e = veng(eng)
        if op1 is None:
            return S.op(eng, lambda: e.tensor_scalar(out=out_ap, in0=in0, scalar1=s1, scalar2=None, op0=op0), reads, writes)
        return S.op(eng, lambda: e.tensor_scalar(out=out_ap, in0=in0, scalar1=s1, scalar2=s2, op0=op0, op1=op1), reads, writes)

    def STT(eng, out_ap, in0, scalar, in1, op0, op1, reads, writes):
        e = veng(eng)
        return S.op(eng, lambda: e.scalar_tensor_tensor(out=out_ap, in0=in0, scalar=scalar, in1=in1, op0=op0, op1=op1), reads, writes)

    def CP(eng, out_ap, in_ap, reads, writes):
        if eng == "act":
            return S.op("act", lambda: nc.scalar.copy(out=out_ap, in_=in_ap), reads, writes)
        e = veng(eng)
        return S.op(eng, lambda: e.tensor_copy(out=out_ap, in_=in_ap), reads, writes)

    def MSET(eng, ap, val, writes):
        e = veng(eng)
        return S.op(eng, lambda: e.memset(ap, val), (), writes)

    def rsqrt_mean(eng, out_ap, in_ap, n, reads, writes):
        TS(eng, out_ap, in_ap, 1.0 / n, EPS, ALU.mult, ALU.add, reads, writes)
        ACT(out_ap, out_ap, AF.Sqrt, writes, writes)
        S.op("dve", lambda: nc.vector.reciprocal(out=out_ap, in_=out_ap), writes, writes)

    es_ph[0] = es
    ident_bf = sb("ident_bf", [128, 128], BF16)
    ident_f = sb("ident_f", [128, 128], F32)
    ones_bf = sb("ones_bf", [128, 128], BF16)
    ones_f = sb("ones_f", [128, 128], F32)
    colv = sb("colv", [128, NCV], F32)
    flag = sb("flag", [128, 2], F32)
    iota_p = sb("iota_p", [128, 1], F32)
    iota_f = sb("iota_f", [128, 512], F32)
    S.op("pool", lambda: nc.gpsimd.iota(iota_p.t[:], pattern=[[0, 1]], base=0, channel_multiplier=1,
                                         allow_small_or_imprecise_dtypes=True), (), [iota_p.b])
    S.op("pool", lambda: nc.gpsimd.iota(iota_f.t[:], pattern=[[1, 512]], base=0, channel_multiplier=0,
                                         allow_small_or_imprecise_dtypes=True), (), [iota_f.b])
    TS("dve", ident_f.t[:], iota_f.t[:, 0:128], iota_p.t[:, 0:1], None, ALU.is_equal, None, [iota_f.b, iota_p.b], [ident_f.b])
    CP("dve", ident_bf.t[:], ident_f.t[:], [ident_f.b], [ident_bf.b])
    MSET("dve", ones_f.t[:], 1.0, [ones_f.b])
    MSET("dve", ones_bf.t[:], 1.0, [ones_bf.b])
    DMA("sp", colv.t[:], colv_d[:, :], (), [colv.b], colv)
    DMA("sp", flag.t[:], flag_d[:, :], (), [flag.b], flag)

    def cvc(name, i=0):
        c = CV[name] + i
        return colv.t[:, c:c + 1]

    def fm_norm_epilogue(pst, n, gain_ap, out_tl, sq_tl, ss_ps, r_tl, ndim=128):
        ACT(sq_tl.t[:, :n], pst.t[:, :n], AF.Square, [pst.b], [sq_tl.b])

        def rest():
            MM(ss_ps.t[:, :n], ones_bf.t[:], sq_tl.t[:, :n], True, True, [ones_bf.b, sq_tl.b], [ss_ps.b])
            rsqrt_mean("dve", r_tl.t[:, :n], ss_ps.t[:, :n], ndim, [ss_ps.b], [r_tl.b])
            STT("dve", out_tl.t[:, :n], pst.t[:, :n], gain_ap, r_tl.t[:, :n], ALU.mult, ALU.mult,
                [pst.b, colv.b, r_tl.b], [out_tl.b])
        return rest

    segs = []
    for i in range(4): segs.append((0 + 512 * i, 512, "q", i))
    for i in range(4): segs.append((2048 + 512 * i, 512, "k", i))
    for i in range(4): segs.append((4096 + 512 * i, 512, "v", i))
    for i in range(12): segs.append((6144 + 512 * i, 512, "gqkv", i))
    for i in range(4): segs.append((12288 + 512 * i, 512, "z", i))
    segs.append((14336, 64, "ab", 0))
    segs.append((14400, 512, "xq", 0))
    for i in range(12): segs.append((14912 + 512 * i, 512, "gate", i))

    with ExitStack() as ph:
        es_ph[0] = ph
        gmix_bc = sb("gmix_bc", [128, D], F32)
        DMA("sp", gmix_bc.t[:], rowv_d[0:1, :].partition_broadcast(128), (), [gmix_bc.b], gmix_bc)
        xt = [sb("xt%d" % i, [128, D], F32) for i in range(2)]
        junk = sb("junk", [128, D], BF16)
        ssq = [sb("ssq%d" % i, [128, 1], F32) for i in range(2)]
        hn = [sb("hn%d" % i, [128, D], BF16) for i in range(2)]
        hT = sb("hT", [128, 16, TB], BF16)
        wch = [sb("wch%d" % i, [128, 16, 512], BF16) for i in range(2)]
        stage = [sb("stage%d" % i, [128, 512], F32) for i in range(4)]
        stage_bf = [sb("stagebf%d" % i, [128, 512], BF16) for i in range(4)]
        sqb = [sb("sqb%d" % i, [128, 512], BF16) for i in range(2)]
        rb = [sb("rb%d" % i, [128, 512], F32) for i in range(2)]
        tp_ps = [ps("tp_ps%d" % i, [128, 16, 128], BF16) for i in range(1)]
        mm_ps = [ps("mm_ps%d" % i, [128, 512], F32) for i in range(4)]
        ss_ps = [ps("ss_ps%d" % i, [128, 512], F32) for i in range(2)]
        cnt = {"st": 0, "mm": 0, "ss": 0, "w": 0}
        deferred = []

        def nxt(key, lst):
            i = cnt[key] % len(lst)
            cnt[key] += 1
            return lst[i], i

        for tb in range(NTB):
            for ti in range(TB // 128):
                r0 = tb * TB + ti * 128
                X = xt[ti % 2]
                DMA("sp", X.t[:], x_d[r0:r0 + 128, :], (), [X.b], X)
                sq_ = ssq[ti % 2]
                ACT(junk.t[:], X.t[:], AF.Square, [X.b], [junk.b, sq_.b], accum_out=sq_.t[:, 0:1])
                rsqrt_mean("dve", sq_.t[:, 0:1], sq_.t[:, 0:1], D, [sq_.b], [sq_.b])
                H = hn[ti % 2]
                STT("dve", H.t[:], X.t[:], sq_.t[:, 0:1], gmix_bc.t[:], ALU.mult, ALU.mult, [X.b, sq_.b, gmix_bc.b], [H.b])
                tp = tp_ps[0]
                for k in range(16):
                    TR(tp.t[:, k, :], H.t[:, k * 128:(k + 1) * 128], ident_bf.t[:], [H.b, ident_bf.b], [tp.b])
                CP("act", hT.t[:, :, ti * 128:(ti + 1) * 128], tp.t[:], [tp.b], [hT.b])
            for (c0, ncol, kind, idx) in segs:
                W, _ = nxt("w", wch)
                DMA("pool", W.t[:, :, :ncol], w_in_d[:, c0:c0 + ncol].rearrange("(k p) c -> p k c", p=128),
                    (), [W.b], W)
                if kind in ("q", "k", "gqkv", "xq", "gate"):
                    for cbk in range(ncol // 128):
                        for tg in range(TB // 512):
                            t0 = tb * TB + tg * 512
                            P, _ = nxt("mm", mm_ps)
                            for k in range(16):
                                MM(P.t[:], W.t[:, k, cbk * 128:(cbk + 1) * 128], hT.t[:, k, tg * 512:(tg + 1) * 512],
                                   k == 0, k == 15, [W.b, hT.b], [P.b])
                            while deferred:
                                deferred.pop(0)()
                            ob, si = nxt("st", stage_bf)
                            blk = idx * 4 + cbk
                            if kind in ("q", "k", "xq"):
                                gname = {"q": "da_q_norm", "k": "da_k_norm", "xq": "xa_q_norm"}[kind]
                                SS, j = nxt("ss", ss_ps)
                                rest = fm_norm_epilogue(P, 512, cvc(gname), ob, sqb[j], SS, rb[j])
                                dst = {"q": qnT_s, "k": knT_s, "xq": xqnT_s}[kind]

                                def fin(rest=rest, dst=dst, blk=blk, t0=t0, ob=ob):
                                    rest()
                                    DMA("sp", dst[blk, :, t0:t0 + 512], ob.t[:], [ob.b], [dbuf("d")], ob)
                                deferred.append(fin)
                                continue
                            elif kind == "gqkv":
                                CP("act", ob.t[:], P.t[:], [P.b], [ob.b])
                                dst = qkvT_s
                            else:
                                ACT(ob.t[:], P.t[:], AF.Sigmoid, [P.b, colv.b], [ob.b], bias=cvc("b_gate", blk))
                                dst = gT_s
                            DMA("sp", dst[blk, :, t0:t0 + 512], ob.t[:], [ob.b], [dbuf("d")], ob)
                else:
                    while deferred:
                        deferred.pop(0)()
                    for tt in range(TB // 128):
                        t0 = tb * TB + tt * 128
                        P, _ = nxt("mm", mm_ps)
                        for k in range(16):
                            MM(P.t[:, :ncol], hT.t[:, k, tt * 128:(tt + 1) * 128], W.t[:, k, :ncol],
                               k == 0, k == 15, [W.b, hT.b], [P.b])
                        if kind == "ab":
                            ob, si = nxt("st", stage)
                            CP("act", ob.t[:, :64], P.t[:, :64], [P.b], [ob.b])
                            DMA("sp", ab_s[t0:t0 + 128, :], ob.t[:, :64], [ob.b], [dbuf("d")], ob)
                        else:
                            ob, si = nxt("st", stage_bf)
                            if kind == "v":
                                CP("act", ob.t[:], P.t[:], [P.b], [ob.b])
                                dst = v_s
                            else:
                                ACT(ob.t[:], P.t[:], AF.Silu, [P.b], [ob.b])
                                dst = z_s
                            DMA("sp", dst[t0:t0 + 128, idx * 512:(idx + 1) * 512], ob.t[:], [ob.b], [dbuf("d")], ob)
            while deferred:
                deferred.pop(0)()
        S.barrier()
    es_ph[0] = es

    WST = object()
    wbufs = []
    wbf = {}
    for nm_, src_, rows_, cols_ in (("p_attn", p_attn_d, D, D), ("p_gdn", p_gdn_d, D, D), ("p_mem", p_mem_d, 512, D),
                                    ("w_o", w_o_d, D, D), ("w_up", w_up_d, D, 2 * DFF), ("w_down", w_dn_d, DFF, D)):
        dst_ = nc.dram_tensor("wbf_" + nm_, [rows_, cols_], BF16, kind="Internal").ap()
        wbf[nm_] = dst_
        for r_ in range(0, rows_, 512):
            r1_ = min(rows_, r_ + 512)
            b_ = dbuf("w")
            wbufs.append(b_)
            DMA("pool", dst_[r_:r1_, :], src_[r_:r1_, :], [], [b_], WST)
    p_attn_d, p_gdn_d, p_mem_d, w_o_d, w_up_d, w_dn_d = (wbf["p_attn"], wbf["p_gdn"], wbf["p_mem"], wbf["w_o"],
                                                         wbf["w_up"], wbf["w_down"])

    scale = 128.0 ** -0.5
    with ExitStack() as ph:
        es_ph[0] = ph
        Bp = sb("Bp", [128, 512], F32)
        Bd = [sb("Bd%d" % c, [128, 512], F32) for c in range(4)]
        TS("dve", Bp.t[:], iota_f.t[:], iota_p.t[:, 0:1], None, ALU.subtract, None, [iota_f.b, iota_p.b], [Bp.b])
        negb = sb("negb", [128, 512], F32)
        for c in range(4):
            TS("dve", Bd[c].t[:], Bp.t[:], float(-128 * c), None, ALU.add, None, [Bp.b], [Bd[c].b])
            TS("dve", negb.t[:], Bd[c].t[:], -1.0, None, ALU.mult, None, [Bd[c].b], [negb.b])
            TT("dve", Bd[c].t[:], Bd[c].t[:], negb.t[:], ALU.max, [Bd[c].b, negb.b], [Bd[c].b])
        lam_t = sb("lam_t", [128, 4], F32)
        neglam = sb("neglam", [128, 1], F32)
        subg = sb("subg", [128, 2], F32)
        NSL, LA = 5, 4
        s_ps = [ps("s_ps%d" % i, [128, 512], F32) for i in range(NSL)]
        lam_ps = s_ps[0]
        c_l = CV["da_lambda"]
        TT("dve", lam_t.t[:, 0:1], colv.t[:, c_l:c_l + 1], colv.t[:, c_l + 1:c_l + 2], ALU.mult, [colv.b], [lam_t.b])
        TT("dve", lam_t.t[:, 1:2], colv.t[:, c_l + 2:c_l + 3], colv.t[:, c_l + 3:c_l + 4], ALU.mult, [colv.b, lam_t.b], [lam_t.b])
        MM(lam_ps.t[:, 0:2], ones_f.t[:], lam_t.t[:, 0:2], True, True, [ones_f.b, lam_t.b], [lam_ps.b])
        ACT(lam_t.t[:, 2:4], lam_ps.t[:, 0:2], AF.Exp, [lam_ps.b, lam_t.b], [lam_t.b])
        STT("dve", neglam.t[:], lam_t.t[:, 3:4], -LAMBDA_INIT, lam_t.t[:, 2:3], ALU.add, ALU.subtract, [lam_t.b], [neglam.b])
        c_s = CV["da_subln"]
        TS("dve", subg.t[:], colv.t[:, c_s:c_s + 2], 1.0 - LAMBDA_INIT, None, ALU.mult, None, [colv.b], [subg.b])

        qT = [sb("qT%d" % m, [128, T], BF16) for m in range(2)]
        kT = [sb("kT%d" % m, [128, T], BF16) for m in range(2)]
        Vh = sb("Vh", [128, NKC, 256], BF16)
        cbh = sb("cbh", [128, NG * NKC], F32)
        tmpb = [sb("tmpb%d" % i, [128, 512], F32) for i in range(4)]
        Pb = [sb("Pb%d" % i, [128, 512], BF16) for i in range(4)]
        o_ps = [[ps("o_ps%d_%d" % (i, j), [128, 512], F32) for j in range(3)] for i in range(1)]
        scnt = [0]
        sq_slots = []
        rl = sb("rl", [128, 512], F32)
        om = [[sb("om%d_%d" % (m, j), [128, 512], F32) for j in range(2)] for m in range(2)]
        dd = [sb("dd%d" % j, [128, 512], F32) for j in range(2)]
        sq2 = [sb("sq2_%d" % j, [128, 512], F32) for j in range(2)]
        r2 = sb("r2", [128, 512], F32)
        ao = [sb("ao%d" % j, [128, 512], BF16) for j in range(4)]
        it = 0
        oset = 0
        aoc = 0
        for h in range(8):
            DMA("sp", cbh.t[:], cb_d[h, :, :], [], [cbh.b], cbh)
            DMA("sp", qT[0].t[:], qnT_s[2 * h, :, :], [], [qT[0].b], qT[0])
            DMA("sp", kT[0].t[:], knT_s[2 * h, :, :], [], [kT[0].b], kT[0])
            DMA("sp", Vh.t[:], v_s[:, h * 256:(h + 1) * 256].rearrange("(kc p) c -> p kc c", p=128), [], [Vh.b], Vh)
            DMA("sp", qT[1].t[:], qnT_s[2 * h + 1, :, :], [], [qT[1].b], qT[1])
            DMA("sp", kT[1].t[:], knT_s[2 * h + 1, :, :], [], [kT[1].b], kT[1])
            sl = SLOPES[h]
            for g in range(NG):
                for m in range(2):
                    OS = o_ps[0]
                    oset += 1
                    def emit_S(kc_):
                        SPx = s_ps[scnt[0] % NSL]
                        scnt[0] += 1
                        MM(SPx.t[:], kT[m].t[:, kc_ * 128:(kc_ + 1) * 128], qT[m].t[:, g * 512:(g + 1) * 512], True, True,
                           [kT[m].b, qT[m].b], [SPx.b])
                        sq_slots.append(SPx)
                    act_kc = []
                    for kc_ in range(NKC):
                        q_lo, q_hi = g * 512, g * 512 + 511
                        k_lo, k_hi = kc_ * 128, kc_ * 128 + 127
                        gap = max(0, k_lo - q_hi, q_lo - k_hi)
                        if sl * gap < 150.0:
                            act_kc.append(kc_)
                    n_act = len(act_kc)
                    for j_ in range(min(LA, n_act)):
                        emit_S(act_kc[j_])
                    for ai, kc in enumerate(act_kc):
                        SP_ = sq_slots.pop(0)
                        TM = tmpb[it % 4]
                        PP = Pb[it % 4]
                        it += 1
                        if ai + LA < n_act:
                            emit_S(act_kc[ai + LA])
                        dlt = g * 512 - kc * 128
                        if dlt >= 127:
                            base, mult = Bp, -sl / scale
                        elif dlt <= -511:
                            base, mult = Bp, sl / scale
                        else:
                            base, mult = Bd[(-dlt) // 128], -sl / scale
                        STT("dve", TM.t[:], base.t[:], mult, SP_.t[:], ALU.mult, ALU.add, [base.b, SP_.b], [TM.b])
                        ci = g * NKC + kc
                        ACT(PP.t[:], TM.t[:], AF.Exp, [TM.b, cbh.b], [PP.b], bias=cbh.t[:, ci:ci + 1], scale=scale)
                        MM(OS[0].t[:], Vh.t[:, kc, 0:128], PP.t[:], ai == 0, ai == n_act - 1, [Vh.b, PP.b], [OS[0].b])
                        MM(OS[1].t[:], Vh.t[:, kc, 128:256], PP.t[:], ai == 0, ai == n_act - 1, [Vh.b, PP.b], [OS[1].b])
                        MM(OS[2].t[:], ones_bf.t[:], PP.t[:], ai == 0, ai == n_act - 1, [ones_bf.b, PP.b], [OS[2].b])
                    S.op("dve", lambda rl=rl, OS=OS: nc.vector.reciprocal(out=rl.t[:], in_=OS[2].t[:]), [OS[2].b], [rl.b])
                    for j in range(2):
                        TT("dve", om[m][j].t[:], OS[j].t[:], rl.t[:], ALU.mult, [OS[j].b, rl.b], [om[m][j].b])
                for j in range(2):
                    STT("dve", dd[j].t[:], om[1][j].t[:], neglam.t[:, 0:1], om[0][j].t[:], ALU.mult, ALU.add,
                        [om[1][j].b, om[0][j].b, neglam.b], [dd[j].b])
                    ACT(sq2[j].t[:], dd[j].t[:], AF.Square, [dd[j].b], [sq2[j].b])
                SSP = s_ps[scnt[0] % NSL]
                scnt[0] += 1
                for j in range(2):
                    MM(SSP.t[:], ones_f.t[:], sq2[j].t[:], j == 0, j == 1, [ones_f.b, sq2[j].b], [SSP.b])
                rsqrt_mean("dve", r2.t[:], SSP.t[:], 256, [SSP.b], [r2.b])
                for j in range(2):
                    A = ao[aoc % 4]
                    aoc += 1
                    STT("dve", A.t[:], dd[j].t[:], subg.t[:, j:j + 1], r2.t[:], ALU.mult, ALU.mult, [dd[j].b, subg.b, r2.b], [A.b])
                    DMA("sp", aT_s[2 * h + j, :, g * 512:(g + 1) * 512], A.t[:], [A.b], [dbuf("d")], A)
        S.barrier()
    es_ph[0] = es

    with ExitStack() as ph:
        es_ph[0] = ph
        gmem_bc = sb("gmem_bc", [128, D], F32)
        DMA("sp", gmem_bc.t[:], rowv_d[1:2, :].partition_broadcast(128), (), [gmem_bc.b], gmem_bc)
        xt = [sb("xt%d" % i, [128, D], F32) for i in range(2)]
        junk = sb("junk", [128, D], BF16)
        ssq = [sb("ssq%d" % i, [128, 1], F32) for i in range(2)]
        hn = [sb("hn%d" % i, [128, D], BF16) for i in range(2)]
        memT = sb("memT", [128, 16, 512], BF16)
        wkv = [sb("wkv%d" % i, [128, 16, 512], BF16) for i in range(2)]
        mkT = sb("mkT", [128, 4, 512], BF16)
        mv = sb("mv", [128, 4, 512], BF16)
        tp = ps("tp_ps", [128, 16, 128], BF16)
        mmp = [ps("mmp%d" % i, [128, 512], F32) for i in range(2)]
        ssp = ps("ssp", [128, 512], F32)
        op_ = [ps("op%d" % i, [128, 512], F32) for i in range(2)]
        lp_ = [ps("lp", [128, 512], F32)] * 2
        sqx = sb("sqx", [128, 512], F32)
        rx = sb("rx", [128, 512], F32)
        for ti in range(4):
            hf, mt = ti // 2, ti % 2
            X = xt[ti % 2]
            DMA("sp", X.t[:], mem_d[hf, mt * 128:(mt + 1) * 128, :], (), [X.b], X)
            sq_ = ssq[ti % 2]
            ACT(junk.t[:], X.t[:], AF.Square, [X.b], [junk.b, sq_.b], accum_out=sq_.t[:, 0:1])
            rsqrt_mean("dve", sq_.t[:, 0:1], sq_.t[:, 0:1], D, [sq_.b], [sq_.b])
            H = hn[ti % 2]
            STT("dve", H.t[:], X.t[:], sq_.t[:, 0:1], gmem_bc.t[:], ALU.mult, ALU.mult, [X.b, sq_.b, gmem_bc.b], [H.b])
            for k in range(16):
                TR(tp.t[:, k, :], H.t[:, k * 128:(k + 1) * 128], ident_bf.t[:], [H.b, ident_bf.b], [tp.b])
            CP("act", memT.t[:, :, ti * 128:(ti + 1) * 128], tp.t[:], [tp.b], [memT.b])
        for c in range(2):
            DMA("pool", wkv[c].t[:], w_kv_d[:, c * 512:(c + 1) * 512].rearrange("(k p) c -> p k c", p=128), (), [wkv[c].b], wkv[c])
        for hh in range(4):
            P = mmp[hh % 2]
            for k in range(16):
                MM(P.t[:], wkv[0].t[:, k, hh * 128:(hh + 1) * 128], memT.t[:, k, :], k == 0, k == 15, [wkv[0].b, memT.b], [P.b])
            ob = Tl(mkT.t, "x")
            ob.b = mkT.b
            ACT(sqx.t[:], P.t[:], AF.Square, [P.b], [sqx.b])
            MM(ssp.t[:], ones_f.t[:], sqx.t[:], True, True, [ones_f.b, sqx.b], [ssp.b])
            rsqrt_mean("dve", rx.t[:], ssp.t[:], 128, [ssp.b], [rx.b])
            STT("dve", mkT.t[:, hh, :], P.t[:], cvc("xa_k_norm"), rx.t[:], ALU.mult, ALU.mult, [P.b, colv.b, rx.b], [mkT.b])
        for ti in range(4):
            P = mmp[ti % 2]
            for k in range(16):
                MM(P.t[:], memT.t[:, k, ti * 128:(ti + 1) * 128], wkv[1].t[:, k, :], k == 0, k == 15, [wkv[1].b, memT.b], [P.b])
            CP("act", mv.t[:, ti, :], P.t[:], [P.b], [mv.b])
        xq = [sb("xq%d" % i, [128, T], BF16) for i in range(2)]
        Px = [sb("Px%d" % i, [128, 512], BF16) for i in range(3)]
        rlx = sb("rlx", [128, 512], F32)
        co = [sb("co%d" % i, [128, 512], BF16) for i in range(2)]
        it = 0
        for hh in range(4):
            XQ = xq[hh % 2]
            DMA("sp", XQ.t[:], xqnT_s[hh, :, :], [], [XQ.b], XQ)
            for g in range(NG):
                hf = g // NGH
                OP, LP = op_[g % 2], lp_[g % 2]
                for mt in range(2):
                    SP_ = mmp[it % 2]
                    PP = Px[it % 3]
                    it += 1
                    c0 = hf * 256 + mt * 128
                    MM(SP_.t[:], mkT.t[:, hh, c0:c0 + 128], XQ.t[:, g * 512:(g + 1) * 512], True, True, [mkT.b, XQ.b], [SP_.b])
                    ACT(PP.t[:], SP_.t[:], AF.Exp, [SP_.b], [PP.b], scale=scale)
                    MM(OP.t[:], mv.t[:, hf * 2 + mt, hh * 128:(hh + 1) * 128], PP.t[:], mt == 0, mt == 1, [mv.b, PP.b], [OP.b])
                    MM(LP.t[:], ones_bf.t[:], PP.t[:], mt == 0, mt == 1, [ones_bf.b, PP.b], [LP.b])
                S.op("dve", lambda rlx=rlx, LP=LP: nc.vector.reciprocal(out=rlx.t[:], in_=LP.t[:]), [LP.b], [rlx.b])
                C = co[g % 2]
                TT("dve", C.t[:], OP.t[:], rlx.t[:], ALU.mult, [OP.b, rlx.b], [C.b])
                DMA("sp", cT_s[hh, :, g * 512:(g + 1) * 512], C.t[:], [C.b], [dbuf("d")], C)
        S.barrier()
    es_ph[0] = es

    if do_gdn:
        NCH = T // 64
        qk_s = dscr("qk_s", [NCH, 128, 32, 64], BF16)
        kv_s = dscr("kv_s", [NCH, 64, 2, 16, 128], BF16)
        of_s = dscr("of_s", [T, D], F32)
        gconst_d = din("gconst", [64, 6, 64])
        with ExitStack() as ph:
            es_ph[0] = ph
            XW = SH + 4
            xr = [sb("xr%d" % i, [128, 2 * XW], BF16) for i in range(2)]
            for X in xr:
                MSET("pool", X.t[:, 0:2], 0.0, [X.b])
                MSET("pool", X.t[:, 2 * XW - 2:2 * XW], 0.0, [X.b])
            dg = [sb("dg%d" % i, [128, 5, 128], BF16) for i in range(2)]
            sv = [sb("sv%d" % i, [128, 512], F32) for i in range(3)]
            sqv = [sb("sqv%d" % i, [128, 512], BF16) for i in range(3)]
            rv = [sb("rv%d" % i, [128, 512], F32) for i in range(3)]
            obv = [sb("obv%d" % i, [128, 512], BF16) for i in range(4)]
            tbv = [sb("tbv%d" % i, [64, 8, 128], BF16) for i in range(2)]
            cps = [ps("cps%d" % i, [128, 512], F32) for i in range(3)]
            ssp = [ps("ssp%d" % i, [128, 512], F32) for i in range(3)]
            trp = [ps("trp%d" % i, [64, 8, 128], F32) for i in range(1)]
            cw = CV["gdn_w"]
            ic = 0
            oc = 0
            gdef = []

            def post_conv(CP_, SV, SQ, RV, SSP, OB, cb, g, oc_):
                if cb < 32:
                    ACT(SV.t[:], CP_.t[:], AF.Silu, [CP_.b], [SV.b])
                    TT("dve", SQ.t[:], SV.t[:], SV.t[:], ALU.mult, [SV.b], [SQ.b])
                    MM(SSP.t[:], ones_bf.t[:], SQ.t[:], True, True, [ones_bf.b, SQ.b], [SSP.b])
                    rsqrt_mean("dve", RV.t[:], SSP.t[:], 1, [SSP.b], [RV.b])
                    scl = (128.0 ** -0.5) if cb < 16 else 1.0
                    STT("dve", OB.t[:], SV.t[:], scl, RV.t[:], ALU.mult, ALU.mult, [SV.b, RV.b], [OB.b])
                    DMA("sp", qk_s[g * 8:(g + 1) * 8, :, cb, :].rearrange("n p t -> p n t"),
                        OB.t[:].rearrange("p (n t) -> p n t", n=8), [OB.b], [dbuf("d")], OB)
                else:
                    ACT(OB.t[:], CP_.t[:], AF.Silu, [CP_.b], [OB.b])
                if cb >= 16:
                    kk, hh = (0, cb - 16) if cb < 32 else (1, cb - 32)
                    TP = trp[0]
                    for n in range(8):
                        MM(TP.t[:, n, :], OB.t[:, n * 64:(n + 1) * 64], ident_bf.t[:], True, True, [OB.b, ident_bf.b], [TP.b])
                    TB_ = tbv[oc_ % 2]
                    CP("act", TB_.t[:], TP.t[:], [TP.b], [TB_.b])
                    DMA("sp", kv_s[g * 8:(g + 1) * 8, :, kk, hh, :].rearrange("n p d -> p n d"), TB_.t[:], [TB_.b], [dbuf("d")], TB_)

            for cb in range(48):
                X = xr[cb % 2]
                DGT = dg[cb % 2]
                DMA("sp", X.t[:, 2:XW], qkvT_s[cb, :, 0:SH + 2], [], [X.b], X)
                DMA("sp", X.t[:, XW:2 * XW - 2], qkvT_s[cb, :, SH - 2:T], [], [X.b], X)
                TS("dve", X.t[:, XW - 2:XW + 2], X.t[:, XW - 2:XW + 2], flag.t[:, 0:1], None, ALU.mult, None, [X.b, flag.b], [X.b])
                for j in range(5):
                    TS("dve", DGT.t[:, j, :], ident_f.t[:], colv.t[:, cw + j * 48 + cb:cw + j * 48 + cb + 1], None, ALU.mult, None,
                       [ident_f.b, colv.b], [DGT.b])
                for g in range(NG):
                    hf, gl = g // NGH, g % NGH
                    c0 = hf * XW + 2 + gl * 512
                    CP_ = cps[ic % 3]
                    SV, SQ, RV = sv[ic % 3], sqv[ic % 3], rv[ic % 3]
                    SSP = ssp[ic % 3]
                    ic += 1
                    for j in range(5):
                        MM(CP_.t[:], DGT.t[:, j, :], X.t[:, c0 + j - 2:c0 + j - 2 + 512], j == 0, j == 4, [DGT.b, X.b], [CP_.b])
                    while len(gdef) > 1:
                        gdef.pop(0)()
                    OB = obv[oc % 4]
                    oc += 1
                    gdef.append(lambda CP_=CP_, SV=SV, SQ=SQ, RV=RV, SSP=SSP, OB=OB, cb=cb, g=g, oc_=oc: post_conv(CP_, SV, SQ, RV, SSP, OB, cb, g, oc_))
                    continue
                    if cb < 32:
                        ACT(SV.t[:], CP_.t[:], AF.Silu, [CP_.b], [SV.b])
                        TT("dve", SQ.t[:], SV.t[:], SV.t[:], ALU.mult, [SV.b], [SQ.b])
                        MM(SSP.t[:], ones_bf.t[:], SQ.t[:], True, True, [ones_bf.b, SQ.b], [SSP.b])
                        rsqrt_mean("dve", RV.t[:], SSP.t[:], 1, [SSP.b], [RV.b])
                        scl = (128.0 ** -0.5) if cb < 16 else 1.0
                        STT("dve", OB.t[:], SV.t[:], scl, RV.t[:], ALU.mult, ALU.mult, [SV.b, RV.b], [OB.b])
                        DMA("sp", qk_s[g * 8:(g + 1) * 8, :, cb, :].rearrange("n p t -> p n t"),
                            OB.t[:].rearrange("p (n t) -> p n t", n=8), [OB.b], [dbuf("d")], OB)
                    else:
                        ACT(OB.t[:], CP_.t[:], AF.Silu, [CP_.b], [OB.b])
                    if cb >= 16:
                        kk, hh = (0, cb - 16) if cb < 32 else (1, cb - 32)
                        TP = trp[0]
                        for n in range(8):
                            MM(TP.t[:, n, :], OB.t[:, n * 64:(n + 1) * 64], ident_bf.t[:], True, True, [OB.b, ident_bf.b], [TP.b])
                        TB_ = tbv[oc % 2]
                        CP("act", TB_.t[:], TP.t[:], [TP.b], [TB_.b])
                        DMA("sp", kv_s[g * 8:(g + 1) * 8, :, kk, hh, :].rearrange("n p d -> p n d"), TB_.t[:], [TB_.b], [dbuf("d")], TB_)
            while gdef:
                gdef.pop(0)()
            S.barrier()
        es_ph[0] = es

        with ExitStack() as ph:
            es_ph[0] = ph
            gcn = sb("gcn", [64, 6, 64], F32)
            DMA("sp", gcn.t[:], gconst_d[:, :, :], [], [gcn.b], gcn)
            gnorm_bc = sb("gnorm_bc", [64, 128], F32)
            alog_bc = sb("alog_bc", [64, 32], F32)
            dtb_bc = sb("dtb_bc", [64, 32], F32)
            DMA("sp", gnorm_bc.t[:], rowv_d[3:4, 0:128].partition_broadcast(64), [], [gnorm_bc.b], gnorm_bc)
            DMA("sp", alog_bc.t[:], rowv_d[4:5, 0:32].partition_broadcast(64), [], [alog_bc.b], alog_bc)
            DMA("sp", dtb_bc.t[:], rowv_d[5:6, 0:32].partition_broadcast(64), [], [dtb_bc.b], dtb_bc)
            negA = sb("negA", [64, 32], F32)
            ACT(negA.t[:], alog_bc.t[:], AF.Exp, [alog_bc.b], [negA.b])
            TS("dve", negA.t[:], negA.t[:], -1.0, None, ALU.mult, None, [negA.b], [negA.b])
            negI = sb("negI", [64, 64], F32)
            TS("dve", negI.t[:], ident_f.t[0:64, 0:64], -1.0, None, ALU.mult, None, [ident_f.b], [negI.b])
            SHP = [64, 2, 16, 64]
            Itile = sb("Itile", SHP, BF16)
            CP("dve", Itile.t[:], ident_f.t[0:64, 0:64].unsqueeze(1).unsqueeze(1).to_broadcast(SHP), [ident_f.b], [Itile.b])
            I64b = ident_f.t[0:64, 0:64].unsqueeze(1).unsqueeze(1).to_broadcast(SHP)
            gmask = sb("gmask", [64, 4, 64], BF16)
            CP("dve", gmask.t[:], gcn.t[:, 2:6, :], [gcn.b], [gmask.b])

            QK = [sb("QK%d" % i, [128, 2, 32, 64], BF16) for i in range(2)]
            KV = sb("KV", [64, 2, 2, 16, 128], BF16)
            ABt = sb("ABt", [64, 2, 64], F32)
            bt = [sb("bt%d" % i, [64, 2, 16], F32) for i in range(2)]
            lnb = sb("lnb", [64, 2, 16], F32)
            gg = sb("gg", [64, 2, 16], F32)
            gc = sb("gc", [64, 2, 16], F32)
            gcb = sb("gcb", [64, 2, 16], F32)
            egc = sb("egc", [64, 2, 16], F32)
            nbeg = [sb("nbeg%d" % i, [64, 2, 16], F32) for i in range(2)]
            kdf = [sb("kdf%d" % i, [64, 2, 16], F32) for i in range(2)]
            egt = [sb("egt%d" % i, [128, 2, 16], F32) for i in range(2)]
            G_sb = sb("G_sb", SHP, BF16)
            QKT_sb = sb("QKT_sb", SHP, BF16)
            Gd = sb("Gd", SHP, F32)
            Gc2 = sb("Gc2", SHP, F32)
            eR = sb("eR", SHP, BF16)
            egrow = sb("egrow", [128, 2, 16, 64], BF16)
            qkT = [sb("qkT%d" % i, SHP, BF16) for i in range(2)]
            Nb = [sb("Nb%d" % i, SHP, BF16) for i in range(2)]
            Mb = [sb("Mb%d" % i, SHP, BF16) for i in range(2)]
            P32 = sb("P32", SHP, F32)
            Pbf = [sb("Pbf%d" % i, SHP, BF16) for i in range(2)]
            qdecT = [sb("qdecT%d" % i, [128, 2, 16, 64], BF16) for i in range(2)]
            kdec = sb("kdec", [64, 2, 16, 128], BF16)
            vb = sb("vb", [64, 2, 16, 128], BF16)
            tmpx_ap = Gc2.t[:].rearrange("p c h i -> p (c h i)").rearrange("p (h d) -> p h d", h=16)
            Xc = sb("Xc", [64, 16, 128], BF16)
            vnew = sb("vnew", [64, 16, 128], BF16)
            S32 = sb("S32", [128, 16, 128], F32)
            Sbf = sb("Sbf", [128, 16, 128], BF16)
            ost = sb("ost", [64, 16, 128], F32)
            OFt = sb("OFt", [64, 16, 128], F32)
            Zt = sb("Zt", [64, 16, 128], BF16)
            ssum = sb("ssum", [64, 16], F32)
            dtok = sb("dtok", [64, 16, 128], BF16)
            dTt = sb("dTt", [128, 16, 64], BF16)
            slab = [ps("slabA", [128, 2048], F32), ps("slabB", [128, 2048], F32)]
            PSL, SSL = slab[0], slab[1]

            def v4(tl, np_=64):
                return tl.t[0:np_, :].rearrange("p (c h i) -> p c h i", c=2, h=16)

            def v3(tl, np_=64):
                return tl.t[0:np_, :].rearrange("p (h d) -> p h d", h=16)

            def bc3(tl):
                return tl.t[:, :, :].unsqueeze(3).to_broadcast(SHP)

            def flat(tl):
                return tl.t[:].rearrange("p c h i -> p (c h i)")

            ones64 = ones_f.t[0:64, 0:64]

            def build_R(dst, d1, mask_idx):
                TT("dve", Gd.t[:], bc3(d1), I64b, ALU.mult, [d1.b, ident_f.b], [Gd.b])
                CP("dve", Gc2.t[:], bc3(gc), [gc.b], [Gc2.b])
                for q4 in range(4):
                    cs = slice(q4 * 512, (q4 + 1) * 512)
                    MM(dst.t[0:64, cs], ones64, flat(Gd)[:, cs], True, False, [ones_f.b, Gd.b], [dst.b])
                    MM(dst.t[0:64, cs], negI.t[:], flat(Gc2)[:, cs], False, False, [negI.b, Gc2.b], [dst.b])
                    MM(dst.t[0:64, cs], gmask.t[:, mask_idx - 2, :], flat(Itile)[:, cs], False, True, [gmask.b, Itile.b], [dst.b])

            def mm32(dst, lhs, rhs):
                for c in range(2):
                    for h in range(16):
                        o_ = (c * 16 + h) * 64
                        MM(dst.t[0:64, o_:o_ + 64], lhs.t[:, c, h, :], rhs.t[:, c, h, :], True, True, [lhs.b, rhs.b], [dst.b])

            def prep(n, p, dr):
                QKp = QK[p]
                DMA("sp", QKp.t[:], qk_s[2 * n:2 * n + 2, :, :, :].rearrange("c p b t -> p c b t"), [], [QKp.b], QKp)
                DMA("sp", ABt.t[:], ab_s[n * 128:(n + 1) * 128, :].rearrange("(c p) x -> p c x", p=64), [], [ABt.b], ABt)
                BT, NBG, KDF, EGT = bt[p], nbeg[p], kdf[p], egt[p]
                ACT(BT.t[:], ABt.t[:, :, dr * 16:(dr + 1) * 16], AF.Sigmoid, [ABt.b], [BT.b])
                ACT(lnb.t[:], BT.t[:], AF.Ln, [BT.b], [lnb.b])
                TT("dve", gg.t[:], ABt.t[:, :, 32 + dr * 16:32 + (dr + 1) * 16],
                   dtb_bc.t[:, dr * 16:(dr + 1) * 16].unsqueeze(1).to_broadcast([64, 2, 16]), ALU.add, [ABt.b, dtb_bc.b], [gg.b])
                ACT(gg.t[:], gg.t[:], AF.Exp, [gg.b], [gg.b])
                ACT(gg.t[:], gg.t[:], AF.Ln, [gg.b, ones_f.b], [gg.b], bias=ones_f.t[0:64, 0:1])
                TT("dve", gg.t[:], gg.t[:], negA.t[:, dr * 16:(dr + 1) * 16].unsqueeze(1).to_broadcast([64, 2, 16]), ALU.mult,
                   [gg.b, negA.b], [gg.b])
                yield
                GP = PSL
                ggf = gg.t[:].rearrange("p c h -> p (c h)")
                MM(GP.t[0:64, 0:32], gcn.t[:, dr, :], ggf, True, True, [gcn.b, gg.b], [GP.b])
                MM(GP.t[:, 32:64], ones_f.t[0:64, :], ggf, True, True, [ones_f.b, gg.b], [GP.b])
                CP("dve", gc.t[:].rearrange("p c h -> p (c h)"), GP.t[0:64, 0:32], [GP.b], [gc.b])
                TT("dve", gcb.t[:], gc.t[:], lnb.t[:], ALU.add, [gc.b, lnb.b], [gcb.b])
                ACT(egc.t[:], gc.t[:], AF.Exp, [gc.b], [egc.b])
                STT("dve", NBG.t[:], egc.t[:], -1.0, BT.t[:], ALU.mult, ALU.mult, [egc.b, BT.b], [NBG.b])
                TT("dve", KDF.t[:].rearrange("p c h -> p (c h)"), GP.t[0:64, 32:64], gc.t[:].rearrange("p c h -> p (c h)"),
                   ALU.subtract, [GP.b, gc.b], [KDF.b])
                ACT(KDF.t[:], KDF.t[:], AF.Exp, [KDF.b], [KDF.b])
                ACT(EGT.t[:].rearrange("p c h -> p (c h)"), GP.t[:, 32:64], AF.Exp, [GP.b], [EGT.b])
                yield
                for c in range(2):
                    for h in range(16):
                        o_ = (c * 16 + h) * 64
                        MM(PSL.t[0:64, o_:o_ + 64], QKp.t[:, c, 16 + h, :], QKp.t[:, c, 16 + h, :], True, True, [QKp.b], [PSL.b])
                CP("act", G_sb.t[:], v4(PSL), [PSL.b], [G_sb.b])
                yield
                for c in range(2):
                    for h in range(16):
                        o_ = (c * 16 + h) * 64
                        MM(PSL.t[0:64, o_:o_ + 64], QKp.t[:, c, 16 + h, :], QKp.t[:, c, h, :], True, True, [QKp.b], [PSL.b])
                CP("act", QKT_sb.t[:], v4(PSL), [PSL.b], [QKT_sb.b])
                yield
                TT("dve", eR.t[:], bc3(egc), I64b, ALU.mult, [egc.b, ident_f.b], [eR.b])
                for q4 in range(4):
                    cs = slice(q4 * 512, (q4 + 1) * 512)
                    MM(PSL.t[:, cs], ones_bf.t[0:64, :], flat(eR)[:, cs], True, True, [ones_bf.b, eR.b], [PSL.b])
                CP("act", egrow.t[:], v4(PSL, 128), [PSL.b], [egrow.b])
                TT("dve", qdecT[p].t[:], QKp.t[:, :, 0:16, :], egrow.t[:], ALU.mult, [QKp.b, egrow.b], [qdecT[p].b])
                yield
                build_R(PSL, gc, 2 + dr)
                ACT(eR.t[:], v4(PSL), AF.Exp, [PSL.b], [eR.b])
                TT("dve", qkT[p].t[:], QKT_sb.t[:], eR.t[:], ALU.mult, [QKT_sb.b, eR.b], [qkT[p].b])
                yield
                build_R(PSL, gcb, 4 + dr)
                ACT(eR.t[:], v4(PSL), AF.Exp, [PSL.b], [eR.b])
                STT("dve", Nb[0].t[:], G_sb.t[:], -1.0, eR.t[:], ALU.mult, ALU.mult, [G_sb.b, eR.b], [Nb[0].b])
                TT("dve", P32.t[:], Nb[0].t[:], I64b, ALU.add, [Nb[0].b, ident_f.b], [P32.b])
                PB = Pbf[p]
                CP("act", PB.t[:], P32.t[:], [P32.b], [PB.b])
                yield
                for c in range(2):
                    for h in range(16):
                        o_ = (c * 16 + h) * 64
                        MM(PSL.t[0:64, o_:o_ + 64], Nb[0].t[:, c, h, :], ident_bf.t[0:64, 0:64], True, True, [Nb[0].b, ident_bf.b], [PSL.b])
                CP("act", Mb[0].t[:], v4(PSL), [PSL.b], [Mb[0].b])
                yield
                for k in range(1, 6):
                    Np, Mp = Nb[(k - 1) % 2], Mb[(k - 1) % 2]
                    Nn, Mn = Nb[k % 2], Mb[k % 2]
                    mm32(PSL, Np, Mp)
                    CP("act", Mn.t[:], v4(PSL), [PSL.b], [Mn.b])
                    yield
                    if k < 5:
                        mm32(PSL, Mp, Np)
                        CP("dve", Nn.t[:], v4(PSL), [PSL.b], [Nn.b])
                        yield
                    mm32(PSL, Mn, PB)
                    TT("dve", P32.t[:], P32.t[:], v4(PSL), ALU.add, [P32.b, PSL.b], [P32.b])
                    CP("act", PB.t[:], P32.t[:], [P32.b], [PB.b])
                    yield

            def steps(n, p, dr):
                QKp, PB, BT, NBG, KDF, EGT = QK[p], Pbf[p], bt[p], nbeg[p], kdf[p], egt[p]
                DMA("sp", KV.t[:], kv_s[2 * n:2 * n + 2, :, :, :, :].rearrange("c p k h d -> p c k h d"), [], [KV.b], KV)
                TT("dve", vb.t[:], KV.t[:, :, 1, :, :], BT.t[:, :, :].unsqueeze(3).to_broadcast([64, 2, 16, 128]), ALU.mult,
                   [KV.b, BT.b], [vb.b])
                TT("dve", kdec.t[:], KV.t[:, :, 0, :, :], KDF.t[:, :, :].unsqueeze(3).to_broadcast([64, 2, 16, 128]), ALU.mult,
                   [KV.b, KDF.b], [kdec.b])
                yield
                for c in ([0, 1] if dr == 0 else [1, 0]):
                    ch = 2 * n + c
                    r0 = ch * 64
                    for h in range(16):
                        MM(SSL.t[0:64, h * 128:(h + 1) * 128], QKp.t[:, c, 16 + h, :], Sbf.t[:, h, :], True, True, [QKp.b, Sbf.b], [SSL.b])
                    TT("dve", tmpx_ap, v3(SSL), NBG.t[:, c, :].unsqueeze(2).to_broadcast([64, 16, 128]), ALU.mult, [SSL.b, NBG.b], [Gc2.b])
                    TT("dve", Xc.t[:], tmpx_ap, vb.t[:, c, :, :], ALU.add, [Gc2.b, vb.b], [Xc.b])
                    yield
                    for h in range(16):
                        MM(SSL.t[0:64, h * 128:(h + 1) * 128], PB.t[:, c, h, :], Xc.t[:, h, :], True, True, [PB.b, Xc.b], [SSL.b])
                    CP("act", vnew.t[:], v3(SSL), [SSL.b], [vnew.b])
                    yield
                    for h in range(16):
                        MM(SSL.t[0:64, h * 128:(h + 1) * 128], qdecT[p].t[:, c, h, :], Sbf.t[:, h, :], True, False, [qdecT[p].b, Sbf.b], [SSL.b])
                        MM(SSL.t[0:64, h * 128:(h + 1) * 128], qkT[p].t[:, c, h, :], vnew.t[:, h, :], False, True, [qkT[p].b, vnew.b], [SSL.b])
                    if dr == 0:
                        CP("act", ost.t[:], v3(SSL), [SSL.b], [ost.b])
                        DMA("sp", of_s[r0:r0 + 64, :], ost.t[:].rearrange("p h d -> p (h d)"), [ost.b], [dbuf("d")], ost)
                        yield
                    else:
                        DMA("sp", OFt.t[:].rearrange("p h d -> p (h d)"), of_s[r0:r0 + 64, :], [], [OFt.b], OFt)
                        DMA("sp", Zt.t[:].rearrange("p h d -> p (h d)"), z_s[r0:r0 + 64, :], [], [Zt.b], Zt)
                        TT("dve", ost.t[:], v3(SSL), OFt.t[:], ALU.add, [SSL.b, OFt.b], [ost.b])
                        yield
                    for h in range(16):
                        MM(SSL.t[:, h * 128:(h + 1) * 128], kdec.t[:, c, h, :], vnew.t[:, h, :], True, True, [kdec.b, vnew.b], [SSL.b])
                    TT("dve", S32.t[:], S32.t[:], EGT.t[:, c, :].unsqueeze(2).to_broadcast([128, 16, 128]), ALU.mult, [S32.b, EGT.b], [S32.b])
                    TT("dve", S32.t[:], S32.t[:], v3(SSL, 128), ALU.add, [S32.b, SSL.b], [S32.b])
                    nxt_ch = ch + 1 if dr == 0 else ch - 1
                    if (dr == 0 and nxt_ch == NCH // 2) or (dr == 1 and ch == NCH // 2):
                        TS("dve", S32.t[:], S32.t[:], flag.t[:, 0:1], None, ALU.mult, None, [S32.b, flag.b], [S32.b])
                    CP("act", Sbf.t[:], S32.t[:], [S32.b], [Sbf.b])
                    yield
                    if dr == 1:
                        TT("dve", tmpx_ap, ost.t[:], ost.t[:], ALU.mult, [ost.b], [Gc2.b])
                        S.op("dve", lambda: nc.vector.reduce_sum(out=ssum.t[:], in_=tmpx_ap, axis=AX.X), [Gc2.b], [ssum.b])
                        rsqrt_mean("dve", ssum.t[:], ssum.t[:], 128, [ssum.b], [ssum.b])
                        TT("dve", ost.t[:], ost.t[:], ssum.t[:, :].unsqueeze(2).to_broadcast([64, 16, 128]), ALU.mult, [ost.b, ssum.b], [ost.b])
                        TT("dve", ost.t[:], ost.t[:], gnorm_bc.t[:, :].unsqueeze(1).to_broadcast([64, 16, 128]), ALU.mult,
                           [ost.b, gnorm_bc.b], [ost.b])
                        TT("dve", dtok.t[:], ost.t[:], Zt.t[:], ALU.mult, [ost.b, Zt.b], [dtok.b])
                        yield
                        for h in range(16):
                            MM(SSL.t[:, h * 64:(h + 1) * 64], dtok.t[:, h, :], ident_bf.t[0:64, 0:64], True, True, [dtok.b, ident_bf.b], [SSL.b])
                        CP("act", dTt.t[:], SSL.t[:, 0:1024].rearrange("p (h t) -> p h t", h=16), [SSL.b], [dTt.b])
                        DMA("sp", dT_s[:, :, r0:r0 + 64].rearrange("h p t -> p h t"), dTt.t[:], [dTt.b], [dbuf("d")], dTt)
                        yield

            def run_gens(*gens):
                alive = [g_ for g_ in gens if g_ is not None]
                while alive:
                    for g_ in list(alive):
                        try:
                            next(g_)
                        except StopIteration:
                            alive.remove(g_)

            for dr in range(2):
                MSET("dve", S32.t[:], 0.0, [S32.b])
                MSET("dve", Sbf.t[:], 0.0, [Sbf.b])
                tiles = list(range(NT)) if dr == 0 else list(range(NT - 1, -1, -1))
                run_gens(prep(tiles[0], 0, dr))
                for ti in range(NT):
                    nxt_prep = prep(tiles[ti + 1], (ti + 1) % 2, dr) if ti + 1 < NT else None
                    run_gens(steps(tiles[ti], ti % 2, dr), nxt_prep)
            S.barrier()
        es_ph[0] = es

    with ExitStack() as ph:
        es_ph[0] = ph
        gffn_bc = sb("gffn_bc", [128, D], F32)
        DMA("sp", gffn_bc.t[:], rowv_d[2:3, :].partition_broadcast(128), (), [gffn_bc.b], gffn_bc)
        aT = sb("aT", [128, 16, 512], BF16)
        dT = sb("dT", [128, 16, 512], BF16)
        cT = sb("cT", [128, 4, 512], BF16)
        gch = [sb("gch%d" % i, [128, 3, 2, 512], BF16) for i in range(2)]
        wpa = [sb("wpa%d" % i, [128, 16, 256], BF16) for i in range(2)]
        wpd = [sb("wpd%d" % i, [128, 16, 256], BF16) for i in range(2)]
        wpc = [sb("wpc%d" % i, [128, 4, 256], BF16) for i in range(2)]
        mT = sb("mT", [128, 16, 512], BF16)
        wo = [sb("wo%d" % i, [128, 16, 256], BF16) for i in range(2)]
        xin = [sb("xin%d" % i, [128, D], F32) for i in range(4)]
        junkf = sb("junkf", [128, D], F32)
        ssq = [sb("ssq%d" % i, [128, 1], F32) for i in range(2)]
        h2 = [sb("h2_%d" % i, [128, D], BF16) for i in range(2)]
        h2T = [sb("h2T%d" % i, [128, 16, 128], BF16) for i in range(2)]
        t1 = [sb("t1_%d" % i, [128, 512], F32) for i in range(2)]
        t2 = [sb("t2_%d" % i, [128, 512], F32) for i in range(2)]
        t3 = [sb("t3_%d" % i, [128, 512], F32) for i in range(2)]
        pa = [ps("pa%d" % i, [128, 512], F32) for i in range(2)]
        pd = ps("pd", [128, 512], F32)
        pc = ps("pc", [128, 512], F32)
        po = [ps("po%d" % i, [128, 256], F32) for i in range(2)]
        tp = ps("tp_ps", [128, 16, 128], BF16)
        wc = 0
        woc = 0
        tcn = 0
        for g in range(NG):
            t0 = g * 512
            DMA("sp", aT.t[:], aT_s[:, :, t0:t0 + 512].rearrange("k p t -> p k t"), [], [aT.b], aT)
            if do_gdn:
                DMA("sp", dT.t[:], dT_s[:, :, t0:t0 + 512].rearrange("k p t -> p k t"), [], [dT.b], dT)
            DMA("sp", cT.t[:], cT_s[:, :, t0:t0 + 512].rearrange("k p t -> p k t"), [], [cT.b], cT)
            for cc in range(8):
                Wa, Wd_, Wc = wpa[wc % 2], wpd[wc % 2], wpc[wc % 2]
                G = gch[wc % 2]
                wc += 1
                cs = slice(cc * 256, (cc + 1) * 256)
                DMA("pool", Wa.t[:], p_attn_d[:, cs].rearrange("(k p) c -> p k c", p=128), wbufs, [Wa.b], Wa)
                if do_gdn:
                    DMA("pool", Wd_.t[:], p_gdn_d[:, cs].rearrange("(k p) c -> p k c", p=128), wbufs, [Wd_.b], Wd_)
                DMA("pool", Wc.t[:], p_mem_d[:, cs].rearrange("(k p) c -> p k c", p=128), wbufs, [Wc.b], Wc)
                for br in range(3):
                    DMA("sp", G.t[:, br, :, :], gT_s[br * 16 + cc * 2: br * 16 + cc * 2 + 2, :, t0:t0 + 512].rearrange("k p t -> p k t"),
                        [], [G.b], G)
                for f in range(2):
                    fc = cc * 2 + f
                    PA = pa[fc % 2]
                    for k in range(16):
                        MM(PA.t[:], Wa.t[:, k, f * 128:(f + 1) * 128], aT.t[:, k, :], k == 0, k == 15, [Wa.b, aT.b], [PA.b])
                    if do_gdn:
                        for k in range(16):
                            MM(pd.t[:], Wd_.t[:, k, f * 128:(f + 1) * 128], dT.t[:, k, :], k == 0, k == 15, [Wd_.b, dT.b], [pd.b])
                    for k in range(4):
                        MM(pc.t[:], Wc.t[:, k, f * 128:(f + 1) * 128], cT.t[:, k, :], k == 0, k == 3, [Wc.b, cT.b], [pc.b])
                    A1, A2, A3 = t1[tcn % 2], t2[tcn % 2], t3[tcn % 2]
                    tcn += 1
                    TT("dve", A1.t[:], PA.t[:], G.t[:, 0, f, :], ALU.mult, [PA.b, G.b], [A1.b])
                    TT("dve", A3.t[:], pc.t[:], G.t[:, 2, f, :], ALU.mult, [pc.b, G.b], [A3.b])
                    if do_gdn:
                        TT("dve", A2.t[:], pd.t[:], G.t[:, 1, f, :], ALU.mult, [pd.b, G.b], [A2.b])
                        TT("pool", A1.t[:], A1.t[:], A2.t[:], ALU.add, [A1.b, A2.b], [A1.b])
                    TT("pool", mT.t[:, fc, :], A1.t[:], A3.t[:], ALU.add, [A1.b, A3.b], [mT.b])
            for tt in range(4):
                r0 = t0 + tt * 128
                DMA("sp", xin[tt].t[:], x_d[r0:r0 + 128, :], [], [xin[tt].b], xin[tt])
            for cc in range(8):
                WO = wo[woc % 2]
                woc += 1
                DMA("pool", WO.t[:], w_o_d[:, cc * 256:(cc + 1) * 256].rearrange("(k p) c -> p k c", p=128), wbufs, [WO.b], WO)
                for tt in range(4):
                    XI = xin[tt]
                    PO = po[(cc * 4 + tt) % 2]
                    for k in range(16):
                        MM(PO.t[:], mT.t[:, k, tt * 128:(tt + 1) * 128], WO.t[:, k, :], k == 0, k == 15, [mT.b, WO.b], [PO.b])
                    TT("dve", XI.t[:, cc * 256:(cc + 1) * 256], PO.t[:], XI.t[:, cc * 256:(cc + 1) * 256], ALU.add, [PO.b, XI.b], [XI.b])
            for tt in range(4):
                r0 = t0 + tt * 128
                X1 = xin[tt]
                DMA("sp", x1_s[r0:r0 + 128, :], X1.t[:], [X1.b], [dbuf("d")], X1)
                sq_ = ssq[tt % 2]
                TT("dve", junkf.t[:], X1.t[:], X1.t[:], ALU.mult, [X1.b], [junkf.b])
                S.op("dve", lambda sq_=sq_: nc.vector.reduce_sum(out=sq_.t[:, 0:1], in_=junkf.t[:], axis=AX.X), [junkf.b], [sq_.b])
                rsqrt_mean("dve", sq_.t[:, 0:1], sq_.t[:, 0:1], D, [sq_.b], [sq_.b])
                H = h2[tt % 2]
                STT("dve", H.t[:], X1.t[:], sq_.t[:, 0:1], gffn_bc.t[:], ALU.mult, ALU.mult, [X1.b, sq_.b, gffn_bc.b], [H.b])
                for k in range(16):
                    TR(tp.t[:, k, :], H.t[:, k * 128:(k + 1) * 128], ident_bf.t[:], [H.b, ident_bf.b], [tp.b])
                HT = h2T[tt % 2]
                CP("act", HT.t[:], tp.t[:], [tp.b], [HT.b])
                DMA("sp", h2T_s[:, :, r0:r0 + 128].rearrange("k p t -> p k t"), HT.t[:], [HT.b], [dbuf("d")], HT)
        S.barrier()
    es_ph[0] = es

    with ExitStack() as ph:
        es_ph[0] = ph
        hT2 = [sb("hT2_%d" % i, [128, 16, 514], BF16) for i in range(1)]
        wg = [sb("wg%d" % i, [128, 16, 256], BF16) for i in range(2)]
        wu = [sb("wu%d" % i, [128, 16, 256], BF16) for i in range(2)]
        actT = sb("actT", [128, 44, 512], BF16)
        gts = [sb("gts%d" % i, [128, 514], F32) for i in range(2)]
        a1 = [sb("a1_%d" % i, [128, 512], F32) for i in range(2)]
        a2 = [sb("a2_%d" % i, [128, 512], F32) for i in range(2)]
        wd = [sb("wd%d" % i, [128, 44, 256], BF16) for i in range(2)]
        x1t = [sb("x1t%d" % i, [128, D], F32) for i in range(4)]
        pg = [ps("pg%d" % i, [128, 512], F32) for i in range(2)]
        pu = [ps("pu%d" % i, [128, 512], F32) for i in range(2)]
        ph_ = [ps("ph%d" % i, [128, 2], F32) for i in range(2)]
        py = [ps("py%d" % i, [128, 256], F32) for i in range(2)]
        wcn = 0
        wdc = 0
        bc = 0
        cw, cbb = CV["ffn_w"], CV["ffn_b"]
        for g in range(NG):
            t0 = g * 512
            HT = hT2[0]
            first = (g % NGH == 0)
            last = (g % NGH == NGH - 1)
            lo = t0 - 1 if t0 > 0 else t0
            hi = t0 + 513 if t0 + 513 <= T else t0 + 512
            DMA("sp", HT.t[:, :, (1 - (t0 - lo)):(1 + hi - t0)], h2T_s[:, :, lo:hi].rearrange("k p t -> p k t"), [], [HT.b], HT)
            if t0 == 0:
                MSET("pool", HT.t[:, :, 0:1], 0.0, [HT.b])
            elif first:
                TS("pool", HT.t[:, :, 0:1], HT.t[:, :, 0:1], flag.t[:, 0:1], None, ALU.mult, None, [HT.b, flag.b], [HT.b])
            if t0 + 512 == T:
                MSET("pool", HT.t[:, :, 513:514], 0.0, [HT.b])
            elif last:
                TS("pool", HT.t[:, :, 513:514], HT.t[:, :, 513:514], flag.t[:, 0:1], None, ALU.mult, None, [HT.b, flag.b], [HT.b])
            for cc in range(22):
                WG, WU = wg[wcn % 2], wu[wcn % 2]
                wcn += 1
                DMA("pool", WG.t[:], w_up_d[:, cc * 256:(cc + 1) * 256].rearrange("(k p) c -> p k c", p=128), wbufs, [WG.b], WG)
                DMA("pool", WU.t[:], w_up_d[:, DFF + cc * 256:DFF + (cc + 1) * 256].rearrange("(k p) c -> p k c", p=128), wbufs, [WU.b], WU)
                for f in range(2):
                    cb_ = cc * 2 + f
                    PG, PU, PH = pg[bc % 2], pu[bc % 2], ph_[bc % 2]
                    GT, A1, A2 = gts[bc % 2], a1[bc % 2], a2[bc % 2]
                    bc += 1
                    for k in range(16):
                        MM(PG.t[:], WG.t[:, k, f * 128:(f + 1) * 128], HT.t[:, k, 1:513], k == 0, k == 15, [WG.b, HT.b], [PG.b])
                    for k in range(16):
                        MM(PH.t[:], WG.t[:, k, f * 128:(f + 1) * 128], HT.t[:, k, 0:514:513], k == 0, k == 15, [WG.b, HT.b], [PH.b])
                    for k in range(16):
                        MM(PU.t[:], WU.t[:, k, f * 128:(f + 1) * 128], HT.t[:, k, 1:513], k == 0, k == 15, [WU.b, HT.b], [PU.b])
                    CP("act", GT.t[:, 1:513], PG.t[:], [PG.b], [GT.b])
                    CP("act", GT.t[:, 0:514:513], PH.t[:], [PH.b], [GT.b])
                    ACT(A1.t[:], GT.t[:, 0:512], AF.Identity, [GT.b, colv.b], [A1.b],
                        bias=colv.t[:, cbb + cb_:cbb + cb_ + 1], scale=colv.t[:, cw + cb_:cw + cb_ + 1])
                    STT("dve", A2.t[:], GT.t[:, 1:513], colv.t[:, cw + 44 + cb_:cw + 44 + cb_ + 1], A1.t[:], ALU.mult, ALU.add,
                        [GT.b, colv.b, A1.b], [A2.b])
                    STT("dve", A1.t[:], GT.t[:, 2:514], colv.t[:, cw + 88 + cb_:cw + 88 + cb_ + 1], A2.t[:], ALU.mult, ALU.add,
                        [GT.b, colv.b, A2.b], [A1.b])
                    ACT(A2.t[:], A1.t[:], AF.Silu, [A1.b], [A2.b])
                    TT("dve", actT.t[:, cb_, :], PU.t[:], A2.t[:], ALU.mult, [PU.b, A2.b], [actT.b])
            for tt in range(4):
                r0 = t0 + tt * 128
                DMA("sp", x1t[tt].t[:], x1_s[r0:r0 + 128, :], [], [x1t[tt].b], x1t[tt])
            for cc in range(8):
                WD = wd[wdc % 2]
                wdc += 1
                DMA("pool", WD.t[:], w_dn_d[:, cc * 256:(cc + 1) * 256].rearrange("(k p) c -> p k c", p=128), wbufs, [WD.b], WD)
                for tt in range(4):
                    X1 = x1t[tt]
                    PY = py[(cc * 4 + tt) % 2]
                    for k in range(44):
                        MM(PY.t[:], actT.t[:, k, tt * 128:(tt + 1) * 128], WD.t[:, k, :], k == 0, k == 43, [actT.b, WD.b], [PY.b])
                    TT("dve", X1.t[:, cc * 256:(cc + 1) * 256], PY.t[:], X1.t[:, cc * 256:(cc + 1) * 256], ALU.add, [PY.b, X1.b], [X1.b])
            for tt in range(4):
                r0 = t0 + tt * 128
                DMA("sp", y_d[r0:r0 + 128, :], x1t[tt].t[:], [x1t[tt].b], [dbuf("d")], x1t[tt])
    es_ph[0] = es
    S.emit()
    es.close()
    return nc, S


def build_gdn(env):
    raise NotImplementedError


def make_cb(SH, joined):
    T = 2 * SH
    NG, NKC = T // 512, T // 128
    cb = np.zeros((8, NG * NKC), np.float32)
    for h in range(8):
        sl = SLOPES[h]
        for g in range(NG):
            for kc in range(NKC):
                dlt = g * 512 - kc * 128
                cross = (g * 512) // SH != (kc * 128) // SH
                if cross and not joined:
                    v = NEG
                elif dlt >= 127 or dlt <= -511:
                    v = -sl * abs(dlt)
                else:
                    v = 0.0
                cb[h, g * NKC + kc] = v
    return np.ascontiguousarray(np.broadcast_to(cb[:, None, :], (8, 128, NG * NKC)))


def make_core_inputs(SH, xs, mems, joined, W):
    colv = np.zeros((128, 480), np.float32)
    c = 0

    def put(v, n):
        nonlocal c
        colv[:, c:c + n] = v
        c += n
    put(W["da_q_norm"].reshape(128, 1), 1)
    put(W["da_k_norm"].reshape(128, 1), 1)
    put(W["xa_q_norm"].reshape(128, 1), 1)
    put(W["xa_k_norm"].reshape(128, 1), 1)
    put(W["da_subln"].reshape(2, 128).T, 2)
    put(W["da_lambda"].reshape(4, 128).T, 4)
    put(W["b_gate"].reshape(48, 128).T, 48)
    put(W["ffn_conv_b"].reshape(44, 128).T, 44)
    put(W["ffn_conv_w"].reshape(3 * 44, 128).T, 132)
    put(W["gdn_conv_w"].reshape(5 * 48, 128).T, 240)
    rowv = np.zeros((8, D), np.float32)
    rowv[0] = W["g_mix"]
    rowv[1] = W["g_mem"]
    rowv[2] = W["g_ffn"]
    rowv[3, :128] = W["gdn_out_norm"]
    rowv[4, :32] = W["gdn_A_log"].reshape(-1)
    rowv[5, :32] = W["gdn_dt_bias"].reshape(-1)
    flag = np.zeros((128, 2), np.float32)
    flag[:, 0] = 1.0 if joined else 0.0
    flag[:, 1] = 0.0 if joined else 1.0
    d = {
        "x": np.ascontiguousarray(xs, dtype=np.float32),
        "mem": np.ascontiguousarray(mems, dtype=np.float32),
        "w_in": W["w_in"], "w_mem_kv": W["w_mem_kv"], "p_attn": W["p_attn"], "p_gdn": W["p_gdn"],
        "p_mem": W["p_mem"], "w_o": W["w_o"], "w_up": W["w_up"], "w_down": W["w_down"],
        "colv": colv, "rowv": rowv, "cb": make_cb(SH, joined), "flag": flag,
    }
    if DO_GDN:
        d["gconst"] = make_gconst()
    return d


def make_gconst():
    i = np.arange(64)[:, None]
    j = np.arange(64)[None, :]
    gcst = np.zeros((64, 6, 64), np.float32)
    gcst[:, 0, :] = (i <= j)
    gcst[:, 1, :] = (i >= j)
    gcst[:, 2, :] = np.where(i >= j, 0.0, NEG)
    gcst[:, 3, :] = np.where(i <= j, 0.0, NEG)
    gcst[:, 4, :] = np.where(i > j, 0.0, NEG)
    gcst[:, 5, :] = np.where(i < j, 0.0, NEG)
    return gcst


WNAMES = ["g_mix", "g_mem", "w_in", "b_gate", "da_q_norm", "da_k_norm", "da_lambda", "da_subln", "gdn_conv_w",
          "gdn_A_log", "gdn_dt_bias", "gdn_out_norm", "xa_q_norm", "xa_k_norm", "w_mem_kv", "p_attn", "p_gdn",
          "p_mem", "w_o", "g_ffn", "w_up", "ffn_conv_w", "ffn_conv_b", "w_down"]

_NC_CACHE = {}
DO_GDN = True


def kernel(x_prompt, x_sample, mem_prompt, mem_sample, **weights):
    SH = 4096
    W = {k: np.ascontiguousarray(np.asarray(weights[k], dtype=np.float32)[0]) for k in WNAMES}
    xp = np.asarray(x_prompt, dtype=np.float32)
    xs = np.asarray(x_sample, dtype=np.float32)
    mp = np.asarray(mem_prompt, dtype=np.float32)
    ms = np.asarray(mem_sample, dtype=np.float32)
    cores = []
    cores.append(make_core_inputs(SH, xp[0:2].reshape(2 * SH, D), mp[0:2], False, W))
    cores.append(make_core_inputs(SH, xp[2:4].reshape(2 * SH, D), mp[2:4], False, W))
    cores.append(make_core_inputs(SH, xs[0], np.stack([ms[0], ms[0]]), True, W))
    cores.append(make_core_inputs(SH, xs[1], np.stack([ms[1], ms[1]]), True, W))
    in_maps = cores + cores
    if SH not in _NC_CACHE:
        _NC_CACHE[SH] = build(SH, do_gdn=DO_GDN)[0]
    nc = _NC_CACHE[SH]
    res = run_bass_kernel_spmd(nc, in_maps, core_ids=list(range(8)))
    ys = [np.asarray(res.results[i]["y"], dtype=np.float32) for i in range(4)]
    y_prompt = np.concatenate([ys[0].reshape(2, SH, D), ys[1].reshape(2, SH, D)], axis=0)
    y_sample = np.stack([ys[2], ys[3]], axis=0)
    return (y_prompt, y_sample)
```
